# Optimizing a Trainium2 kernel written in Bass

```python
import jax, jax.numpy as jnp
from jax import lax
import numpy as np

D_MODEL = 1024
BATCH = 8
SEQ = 4096
DEPTH = 1

HEAD_DIM = 64
SB_HEADS = 8
RWKV_HEADS = 8
SB_WIDTH = SB_HEADS * HEAD_DIM
RWKV_WIDTH = RWKV_HEADS * HEAD_DIM
MIX_WIDTH = SB_WIDTH + RWKV_WIDTH
DECAY_LORA = 64
AAA_LORA = 64
GATE_LORA = 128
D_FF = 2816
Q_BLOCK = 128
NORM_EPS = 1e-6
GN_EPS = HEAD_DIM * 1e-5
SHIFT_WIDTH = 3 * RWKV_WIDTH + DECAY_LORA + AAA_LORA + GATE_LORA
IN_WIDTH = 3 * SB_WIDTH + SHIFT_WIDTH

kernel_name = "hybrid_stickbreaking_rwkv7_macaron_layer"


def rmsnorm(x, g):
    xf = x.astype(jnp.float32)
    y = xf * lax.rsqrt(jnp.mean(xf * xf, axis=-1, keepdims=True) + NORM_EPS)
    return (y * g.astype(jnp.float32)).astype(x.dtype)


def swiglu(x, w_gate, w_up, w_down):
    return (jax.nn.silu(x @ w_gate) * (x @ w_up)) @ w_down


def stick_breaking_attention(q, k, v):
    S = q.shape[2]
    d = q.shape[3]
    scale = d ** -0.5
    outs = []
    for start in range(0, S, Q_BLOCK):
        end = start + Q_BLOCK
        qb = q[:, :, start:end]
        kb = k[:, :, :end]
        vb = v[:, :, :end]
        z = jnp.einsum('bhqd,bhkd->bhqk', qb, kb).astype(jnp.float32) * scale
        t_idx = start + jnp.arange(Q_BLOCK)[:, None]
        s_idx = jnp.arange(end)[None, :]
        causal = s_idx < t_idx
        log_keep = jnp.where(causal, jax.nn.log_sigmoid(-z), 0.0)
        later = lax.cumsum(log_keep, axis=3, reverse=True) - log_keep
        weights = jnp.where(causal, jnp.exp(jax.nn.log_sigmoid(z) + later), 0.0)
        outs.append(jnp.einsum('bhqk,bhkd->bhqd', weights.astype(v.dtype), vb))
    return jnp.concatenate(outs, axis=2)


def rwkv7_scan(r, w, k, v, a_vec, b_vec):
    B, _, H, N = r.shape

    def step(state, inp):
        r_t, w_t, k_t, v_t, a_t, b_t = inp
        sa = jnp.einsum('bhij,bhj->bhi', state, a_t)
        state = (state * w_t[:, :, None, :]
                 + sa[..., None] * b_t[:, :, None, :]
                 + v_t[..., None] * k_t[:, :, None, :])
        y = jnp.einsum('bhij,bhj->bhi', state, r_t)
        return state, y

    xs = tuple(jnp.moveaxis(t.astype(jnp.float32), 1, 0) for t in (r, w, k, v, a_vec, b_vec))
    state0 = jnp.zeros((B, H, N, N), jnp.float32)
    _, ys = lax.scan(step, state0, xs)
    return jnp.moveaxis(ys, 0, 1)


def hybrid_mixer(h, w_in, shift_mu, sb_out_g, decay_w0, decay_w2, iclr_a0, iclr_a2,
                 gate_w2, k_k, k_a, r_k, gn_g, gn_b, w_out):
    B, S, _ = h.shape
    p = h @ w_in
    p_sb = p[..., :3 * SB_WIDTH]
    p_rw = p[..., 3 * SB_WIDTH:]

    q, k, v = jnp.split(p_sb, 3, axis=-1)
    to_heads = lambda t: t.reshape(B, S, SB_HEADS, HEAD_DIM).transpose(0, 2, 1, 3)
    sb = stick_breaking_attention(to_heads(q), to_heads(k), to_heads(v))
    sb = sb.transpose(0, 2, 1, 3)
    sb = rmsnorm(sb, sb_out_g.reshape(SB_HEADS, HEAD_DIM)).reshape(B, S, SB_WIDTH)

    p_prev = jnp.pad(p_rw, ((0, 0), (1, 0), (0, 0)))[:, :-1]
    p_rw = p_rw + (p_prev - p_rw) * shift_mu
    splits = [RWKV_WIDTH, 2 * RWKV_WIDTH, 3 * RWKV_WIDTH,
              3 * RWKV_WIDTH + DECAY_LORA, 3 * RWKV_WIDTH + DECAY_LORA + AAA_LORA]
    r, kr, vr, wd, ad, gd = jnp.split(p_rw, splits, axis=-1)
    w_log = -jax.nn.softplus(-(decay_w0 + jnp.tanh(wd) @ decay_w2)) - 0.5
    decay = jnp.exp(-jnp.exp(w_log.astype(jnp.float32)))
    a = jax.nn.sigmoid(iclr_a0 + ad @ iclr_a2)
    g = jax.nn.sigmoid(gd) @ gate_w2
    heads = lambda t: t.reshape(B, S, RWKV_HEADS, HEAD_DIM)
    kk = heads((kr * k_k).astype(jnp.float32))
    kk = kk / jnp.maximum(jnp.linalg.norm(kk, axis=-1, keepdims=True), 1e-12)
    kr = kr * (1.0 + (a - 1.0) * k_a)
    r_h, k_h, v_h, a_h = heads(r), heads(kr), heads(vr), heads(a)
    y = rwkv7_scan(r_h, heads(decay), k_h, v_h, -kk, kk * a_h.astype(jnp.float32))
    mu = jnp.mean(y, axis=-1, keepdims=True)
    var = jnp.mean(jnp.square(y - mu), axis=-1, keepdims=True)
    y = (y - mu) * lax.rsqrt(var + GN_EPS)
    y = y * gn_g.reshape(RWKV_HEADS, HEAD_DIM).astype(jnp.float32) + gn_b.reshape(RWKV_HEADS, HEAD_DIM).astype(jnp.float32)
    bonus = jnp.sum((r_h * k_h * r_k).astype(jnp.float32), axis=-1, keepdims=True) * v_h.astype(jnp.float32)
    y = (y + bonus).astype(h.dtype).reshape(B, S, RWKV_WIDTH) * g

    return jnp.concatenate([sb, y], axis=-1) @ w_out


def setup_inputs(seed: int = 0) -> dict:
    key = jax.random.key(seed)
    ks = jax.random.split(key, 32)
    L = DEPTH
    nrm = lambda k, shape, s: jax.random.normal(k, shape, jnp.float32) * s
    gain = lambda k, n: 1.0 + nrm(k, (L, n), 0.02)
    return {
        "x": jax.random.normal(ks[0], (BATCH, SEQ, D_MODEL), jnp.float32),
        "ffn1_pre_g": gain(ks[1], D_MODEL),
        "ffn1_post_g": gain(ks[2], D_MODEL),
        "ffn1_w_gate": nrm(ks[3], (L, D_MODEL, D_FF), D_MODEL ** -0.5),
        "ffn1_w_up": nrm(ks[4], (L, D_MODEL, D_FF), D_MODEL ** -0.5),
        "ffn1_w_down": nrm(ks[5], (L, D_FF, D_MODEL), D_FF ** -0.5),
        "mix_pre_g": gain(ks[6], D_MODEL),
        "mix_post_g": gain(ks[7], D_MODEL),
        "w_in": nrm(ks[8], (L, D_MODEL, IN_WIDTH), D_MODEL ** -0.5),
        "shift_mu": jax.random.uniform(ks[9], (L, SHIFT_WIDTH), jnp.float32),
        "sb_out_g": gain(ks[10], SB_WIDTH),
        "decay_w0": jax.random.uniform(ks[11], (L, RWKV_WIDTH), jnp.float32, -6.0, 1.0),
        "decay_w2": nrm(ks[12], (L, DECAY_LORA, RWKV_WIDTH), 0.1 * DECAY_LORA ** -0.5),
        "iclr_a0": nrm(ks[13], (L, RWKV_WIDTH), 0.1),
        "iclr_a2": nrm(ks[14], (L, AAA_LORA, RWKV_WIDTH), 0.1 * AAA_LORA ** -0.5),
        "gate_w2": nrm(ks[15], (L, GATE_LORA, RWKV_WIDTH), GATE_LORA ** -0.5),
        "k_k": 0.85 + nrm(ks[16], (L, RWKV_WIDTH), 0.05),
        "k_a": 1.0 + nrm(ks[17], (L, RWKV_WIDTH), 0.05),
        "r_k": nrm(ks[18], (L, RWKV_HEADS, HEAD_DIM), 0.1),
        "gn_g": gain(ks[19], RWKV_WIDTH),
        "gn_b": nrm(ks[20], (L, RWKV_WIDTH), 0.02),
        "w_out": nrm(ks[21], (L, MIX_WIDTH, D_MODEL), MIX_WIDTH ** -0.5),
        "ffn2_pre_g": gain(ks[22], D_MODEL),
        "ffn2_post_g": gain(ks[23], D_MODEL),
        "ffn2_w_gate": nrm(ks[24], (L, D_MODEL, D_FF), D_MODEL ** -0.5),
        "ffn2_w_up": nrm(ks[25], (L, D_MODEL, D_FF), D_MODEL ** -0.5),
        "ffn2_w_down": nrm(ks[26], (L, D_FF, D_MODEL), D_FF ** -0.5),
    }


def reference(x, ffn1_pre_g, ffn1_post_g, ffn1_w_gate, ffn1_w_up, ffn1_w_down,
              mix_pre_g, mix_post_g, w_in, shift_mu, sb_out_g, decay_w0, decay_w2,
              iclr_a0, iclr_a2, gate_w2, k_k, k_a, r_k, gn_g, gn_b, w_out,
              ffn2_pre_g, ffn2_post_g, ffn2_w_gate, ffn2_w_up, ffn2_w_down):
    for l in range(DEPTH):
        f = swiglu(rmsnorm(x, ffn1_pre_g[l]), ffn1_w_gate[l], ffn1_w_up[l], ffn1_w_down[l])
        x = x + 0.5 * rmsnorm(f, ffn1_post_g[l])
        m = hybrid_mixer(rmsnorm(x, mix_pre_g[l]), w_in[l], shift_mu[l], sb_out_g[l],
                         decay_w0[l], decay_w2[l], iclr_a0[l], iclr_a2[l], gate_w2[l],
                         k_k[l], k_a[l], r_k[l], gn_g[l], gn_b[l], w_out[l])
        x = x + rmsnorm(m, mix_post_g[l])
        f = swiglu(rmsnorm(x, ffn2_pre_g[l]), ffn2_w_gate[l], ffn2_w_up[l], ffn2_w_down[l])
        x = x + 0.5 * rmsnorm(f, ffn2_post_g[l])
    return x
```

```python
import numpy as np
from contextlib import ExitStack
import concourse.bass as bass
import concourse.mybir as mybir
from concourse.bass_utils import run_bass_kernel_spmd

F32 = mybir.dt.float32
BF16 = mybir.dt.bfloat16
AF = mybir.ActivationFunctionType
ALU = mybir.AluOpType
AX = mybir.AxisListType

ENGS = ("pe", "act", "dve", "pool", "sp")

D = 1024
SEQ = 4096
DFF = 2816
NFF = 22
TT = 512
NCH = 8
INW = 3328
EPS = 1e-6
GN_EPS = 64 * 1e-5


class Sched:
    def __init__(self):
        self.streams = {e: [] for e in ENGS}
        self.count = {e: 0 for e in ENGS}
        self.last_w = {}
        self.readers = {}
        self.waited = {e: {} for e in ENGS}
        self.dma_cnt = {}
        self.sem_names = set(ENGS)

    def _deps(self, eng, reads, writes, skip_same):
        deps = {}

        def add(d):
            s, v = d
            if skip_same and s == eng:
                return
            if deps.get(s, 0) < v:
                deps[s] = v

        for r in reads:
            d = self.last_w.get(r)
            if d is not None:
                add(d)
        for w in writes:
            d = self.last_w.get(w)
            if d is not None:
                add(d)
            for s, v in self.readers.get(w, {}).items():
                add((s, v))
        out = []
        wd = self.waited[eng]
        for s, v in deps.items():
            if wd.get(s, 0) >= v:
                continue
            wd[s] = v
            out.append((s, v))
        return out

    def _record(self, tag, reads, writes):
        s, v = tag
        for r in reads:
            rd = self.readers.setdefault(r, {})
            if rd.get(s, 0) < v:
                rd[s] = v
        for w in writes:
            self.last_w[w] = tag
            self.readers[w] = {}

    def begin_record(self):
        self.rec = []

    def end_record(self):
        r, self.rec = self.rec, None
        return r

    COST = {"pe": 0.15, "act": 0.55, "dve": 0.62, "pool": 0.9, "sp": 2.0}

    def replay_merged(self, a, b, cost_a=None, cost_b=None):
        eng_free = {}
        ready_w = {}
        ready_r = {}
        LAT = 0.12

        def est_start(it, cost):
            eng = it[1]
            if it[0] == "op":
                reads, writes = it[3], it[4]
            else:
                reads, writes = it[4], it[5]
            t = eng_free.get(eng, 0.0)
            for r in reads:
                t = max(t, ready_w.get(r, 0.0) + LAT)
            for w in writes:
                t = max(t, ready_w.get(w, 0.0) + LAT, ready_r.get(w, 0.0) + LAT)
            return t

        def commit(it, cost, t0):
            eng = it[1]
            if it[0] == "op":
                reads, writes = it[3], it[4]
                c = (cost or self.COST).get(eng, 0.5)
            else:
                reads, writes = it[4], it[5]
                c = 2.5
            t1 = t0 + c
            eng_free[eng] = t1 if it[0] == "op" else t0 + 0.1
            for r in reads:
                ready_r[r] = max(ready_r.get(r, 0.0), t1)
                if r.startswith("ps"):
                    ready_w[r] = max(ready_w.get(r, 0.0), t1)
            for w in writes:
                ready_w[w] = t1

        i = j = 0
        na, nb = len(a), len(b)
        while i < na or j < nb:
            if j >= nb:
                pick = 0
            elif i >= na:
                pick = 1
            else:
                ta, tb = est_start(a[i], cost_a), est_start(b[j], cost_b)
                if abs(ta - tb) < 1e-9:
                    pick = 0 if i * nb <= j * na else 1
                else:
                    pick = 0 if ta < tb else 1
            if pick == 0:
                it, cost = a[i], cost_a
                i += 1
            else:
                it, cost = b[j], cost_b
                j += 1
            commit(it, cost, est_start(it, cost))
            if it[0] == "op":
                self.op(*it[1:])
            else:
                self.dma(*it[1:])

    def op(self, eng, fn, reads=(), writes=(), inc=True):
        if getattr(self, "rec", None) is not None:
            self.rec.append(("op", eng, fn, list(reads), list(writes)))
            return
        ex = [r for r in reads if r.startswith("ps")]
        if ex:
            reads = [r for r in reads if not r.startswith("ps")]
            writes = list(writes) + ex
        waits = self._deps(eng, reads, writes, skip_same=(eng == "pe"))
        inc = True
        if inc:
            self.count[eng] += 1
            v = self.count[eng]
        else:
            v = self.count[eng] + 1
        self.streams[eng].append((waits, fn, (eng, 1) if inc else None))
        self._record((eng, v), reads, writes)

    def dma(self, queue, fn, key, reads=(), writes=()):
        if getattr(self, "rec", None) is not None:
            self.rec.append(("dma", queue, fn, key, list(reads), list(writes)))
            return
        waits = self._deps(queue, reads, writes, skip_same=False)
        sname = "dma:" + key
        self.sem_names.add(sname)
        self.dma_cnt[sname] = self.dma_cnt.get(sname, 0) + 16
        v = self.dma_cnt[sname]
        self.streams[queue].append((waits, fn, (sname, 16)))
        self._record((sname, v), reads, writes)

    def final_wait(self, eng, bufs):
        waits = self._deps(eng, bufs, (), skip_same=False)
        self.streams[eng].append((waits, None, None))

    def emit(self, nc, stack):
        sems = {}
        for s in sorted(self.sem_names):
            sems[s] = stack.enter_context(nc.semaphore(s.replace(":", "_")))
        block = stack.enter_context(nc.Block())
        streams = self.streams

        def run(e, stream):
            for waits, fn, inc in stream:
                for s, v in waits:
                    e.wait_ge(sems[s], v)
                if fn is None:
                    continue
                ins = fn(e)
                if inc is not None:
                    ins.then_inc(sems[inc[0]], inc[1])

        @block.tensor
        def _(e):
            run(e, streams["pe"])

        @block.scalar
        def _(e):
            run(e, streams["act"])

        @block.vector
        def _(e):
            run(e, streams["dve"])

        @block.gpsimd
        def _(e):
            run(e, streams["pool"])

        @block.sync
        def _(e):
            run(e, streams["sp"])


CST_IDENT, CST_ONES, CST_UI, CST_LT, CST_LE, CST_BONES, CST_GT = range(7)
NCST = 7


def make_consts():
    p = np.arange(128)[:, None]
    j = np.arange(128)[None, :]
    mats = [
        (p == j), np.ones((128, 128), bool), (p >= j), (p < j), (p <= j),
        (p // 64 == j // 64), (p > j),
    ]
    return np.concatenate([m.astype(np.float32) for m in mats], axis=1)


PP = {}


def _pp_layout():
    off = 0
    for name, n in [("ffn1_pre_g", 8), ("ffn1_post_g", 8), ("mix_pre_g", 8), ("mix_post_g", 8),
                    ("ffn2_pre_g", 8), ("ffn2_post_g", 8), ("shift_mu", 14), ("sb_out_g", 4),
                    ("k_k", 4), ("k_a", 4), ("r_k", 4), ("iclr_a0", 4)]:
        PP[name] = (off, n)
        off += n
    return off


NPP = _pp_layout()


def build_nc(n_tiles=8, stage="full", dbg=False):
    nc = bass.Bass("TRN2", target_bir_lowering=False)
    S = Sched()
    dr = lambda n, shp, dt, kind: nc.dram_tensor(n, shp, dt, kind=kind).ap()
    x_d = dr("x", [SEQ, D], F32, "ExternalInput")
    out_d = dr("out", [SEQ, D], F32, "ExternalOutput")
    cst_d = dr("cst", [128, NCST * 128], F32, "ExternalInput")
    pp_d = dr("pp", [128, NPP], F32, "ExternalInput")
    wts = {}
    for f in ("ffn1", "ffn2"):
        wts[f + "_w_gate"] = dr(f + "_w_gate", [D, DFF], F32, "ExternalInput")
        wts[f + "_w_up"] = dr(f + "_w_up", [D, DFF], F32, "ExternalInput")
        wts[f + "_w_down"] = dr(f + "_w_down", [DFF, D], F32, "ExternalInput")
    w_in_d = dr("w_in", [D, INW], F32, "ExternalInput")
    w_out_d = dr("w_out", [D, D], F32, "ExternalInput")
    decay_w2_d = dr("decay_w2", [64, 512], F32, "ExternalInput")
    iclr_a2_d = dr("iclr_a2", [64, 512], F32, "ExternalInput")
    gate_w2_d = dr("gate_w2", [128, 512], F32, "ExternalInput")
    decay_w0_d = dr("decay_w0", [1, 512], F32, "ExternalInput")
    gn_g_d = dr("gn_g", [1, 512], F32, "ExternalInput")
    gn_b_d = dr("gn_b", [1, 512], F32, "ExternalInput")
    dbg_d = dr("dbg", [n_tiles, 128, 8 * TT], BF16, "ExternalOutput") if dbg else None
    scr = {}
    for f in ("ffn1", "ffn2"):
        scr[f + "_g"] = dr(f + "_gs", [11, 128, 2048], BF16, "Internal")
        scr[f + "_u"] = dr(f + "_us", [11, 128, 2048], BF16, "Internal")
        scr[f + "_d"] = dr(f + "_ds", [16, 128, 1408], BF16, "Internal")
    scr["w_in"] = dr("w_in_s", [13, 128, 2048], BF16, "Internal")
    scr["w_out"] = dr("w_out_s", [4, 128, 2048], BF16, "Internal")

    with ExitStack() as st:
        sb = lambda n, shp, dt: st.enter_context(nc.sbuf_tensor(n, shp, dt))
        NSLOT = 4
        cst = sb("cst_sb", [128, NCST * 128], F32)
        cstb = sb("cstb", [128, NCST * 128], BF16)
        pp = sb("ppar", [128, NPP], F32)
        ghalf = sb("ghalf", [128, 24], F32)
        epsT = sb("epsT", [128, 4], F32)
        onem = sb("onem", [128, 14], F32)
        xT = sb("xT_sb", [128, NCH, TT], F32)
        xnT = sb("xnT", [128, NCH, TT], BF16)
        hT = sb("hT", [128, NFF, TT], BF16)
        fT = sb("fT", [128, NCH, TT], F32)
        wslot = [sb("wslot%d" % i, [128, 2048], BF16) for i in range(NSLOT)]
        xio = [sb("xio%d" % i, [128, D], F32) for i in range(2)]
        sqb = [sb("sqb%d" % i, [128, TT], BF16) for i in range(2)]
        silb = [sb("silb%d" % i, [128, TT], BF16) for i in range(2)]
        rstd = sb("rstd", [128, TT], F32)
        tmpf = [sb("tmpf%d" % i, [128, TT], F32) for i in range(2)]
        kTh = sb("kTh", [128, 4, SEQ], BF16)
        dvh = sb("dvh", [128, SEQ // 128, 512], BF16)
        gbuf0 = sb("gbuf0", [128, 4, TT], BF16)
        gbuf1 = sb("gbuf1", [128, 4, TT], BF16)
        mixT = sb("mixT", [128, NCH, TT], BF16)
        halfA = [sb("halfA%d" % i, [128, 4, 128], BF16) for i in range(2)]
        halfB = [sb("halfB%d" % i, [128, 4, 128], BF16) for i in range(2)]
        halfK = [sb("halfK%d" % i, [128, 4, 128], BF16) for i in range(2)]
        Tfin = sb("Tfin", [128, 8, 128], BF16)
        Mbr = sb("Mbr", [128, 8, 128], BF16)
        Mkr = sb("Mkr", [128, 8, 128], BF16)
        pst = sb("pst", [128, TT + 1], F32)
        lora_w = sb("lora_w", [128, 512], BF16)
        gate_wb = sb("gate_wb", [128, 512], BF16)
        w0b = sb("w0b", [128, 512], F32)
        gng = sb("gng", [128, 512], F32)
        gnb = sb("gnb", [128, 512], F32)
        Sf = sb("Sf", [128, 4, 64], F32)
        Sb = sb("Sb", [128, 4, 64], BF16)
        bonus = sb("bonus", [128, 4, 8], F32)
        xn_carry = sb("xn_carry", [128, 8, 1], BF16)
        prw_carry = sb("prw_carry", [128, 14], F32)
        nLgC = sb("nLgC", [128, 4], F32)
        gamC = sb("gamC", [128, 4], F32)
        gst = sb("gst", [128, 4, 8], F32)
        ps = [st.enter_context(nc.psum_tensor("ps%d" % i, [128, 512], F32)) for i in range(8)]
        psb = [p.bitcast(BF16) for p in ps]

        def cI(k):
            return cst[:, k * 128:(k + 1) * 128]

        def cB(k):
            return cstb[:, k * 128:(k + 1) * 128]

        def ppc(name, c):
            o, n = PP[name]
            return pp[:, o + c:o + c + 1]

        def hTc(i):
            return hT[:, i, :]

        def fTc(i):
            return fT[:, i, :]

        S.dma("sp", lambda e: e.dma_start(out=cst[:], in_=cst_d), "cst", writes=["cst"])
        S.dma("sp", lambda e: e.dma_start(out=pp[:], in_=pp_d), "pp", writes=["pp"])
        S.op("dve", lambda e: e.tensor_copy(out=cstb[:], in_=cst[:]), reads=["cst"], writes=["cstb"])
        S.op("dve", lambda e: e.memset(epsT[:, 0:1], EPS), writes=["epsT"])
        S.op("dve", lambda e: e.memset(epsT[:, 1:2], GN_EPS), writes=["epsT"])
        S.op("dve", lambda e: e.memset(epsT[:, 2:3], -0.5), writes=["epsT"])
        S.op("dve", lambda e: e.memset(epsT[:, 3:4], 1.0), writes=["epsT"])
        for i, nm in enumerate(("ffn1_post_g", "mix_post_g", "ffn2_post_g")):
            o, n = PP[nm]
            sc = 1.0 if nm == "mix_post_g" else 0.5
            S.op("dve", (lambda i, o, sc: lambda e: e.tensor_scalar(
                out=ghalf[:, i * 8:(i + 1) * 8], in0=pp[:, o:o + 8], scalar1=sc, scalar2=None, op0=ALU.mult))(i, o, sc),
                reads=["pp"], writes=["ghalf"])
        o_mu = PP["shift_mu"][0]
        S.op("dve", lambda e: e.tensor_scalar(out=onem[:], in0=pp[:, o_mu:o_mu + 14], scalar1=-1.0, scalar2=1.0,
                                              op0=ALU.mult, op1=ALU.add), reads=["pp"], writes=["onem"])
        S.dma("sp", lambda e: e.dma_start(out=xio[1][0:64, 0:512], in_=decay_w2_d), "sw0", writes=["xio1"])
        S.dma("sp", lambda e: e.dma_start(out=xio[1][64:128, 0:512], in_=iclr_a2_d), "sw1", writes=["xio1"])
        S.dma("sp", lambda e: e.dma_start(out=xio[1][:, 512:1024], in_=gate_w2_d), "sw2", writes=["xio1"])
        S.op("dve", lambda e: e.tensor_copy(out=lora_w[:], in_=xio[1][:, 0:512]), reads=["xio1"], writes=["lora_w"])
        S.op("dve", lambda e: e.tensor_copy(out=gate_wb[:], in_=xio[1][:, 512:1024]), reads=["xio1"], writes=["gate_wb"])
        for i, (t, d_, nm) in enumerate(((w0b, decay_w0_d, "w0b"), (gng, gn_g_d, "gng"), (gnb, gn_b_d, "gnb"))):
            src = bass.AP(d_.tensor, 0, [[0, 128], [1, 512]])
            S.dma("sp", (lambda t, src: lambda e: e.dma_start(out=t[:], in_=src))(t, src), "bc%d" % i, writes=[nm])
        S.op("pool", lambda e: e.memset(Sf[:], 0.0), writes=["Sf"])
        S.op("pool", lambda e: e.memset(Sb[:], 0.0), writes=["Sb"])
        S.op("pool", lambda e: e.memset(xn_carry[:], 0.0), writes=["xn_carry"])
        S.op("pool", lambda e: e.memset(prw_carry[:], 0.0), writes=["prw_carry"])

        cast_i = [0]

        def cast_dma(dst_v, src_v, nm):
            k = cast_i[0] % 8
            cast_i[0] += 1
            S.dma("pool", lambda e: e.dma_start(out=dst_v, in_=src_v), "cast%d" % k,
                  writes=["scr_" + nm, "casttok%d" % k])

        def cast_ffn(f):
            for g in range(11):
                for which, key in (("_g", "_w_gate"), ("_u", "_w_up")):
                    src = wts[f + key]
                    src_v = src.rearrange("(c p) f -> p c f", p=128)[:, :, g * 256:(g + 1) * 256]
                    dst_v = scr[f + which][g].rearrange("p (c j) -> p c j", c=8)
                    cast_dma(dst_v, src_v, "%s%s%d" % (f, which, g))
            src = wts[f + "_w_down"]
            for dc in range(8):
                for hf in range(2):
                    src_v = src.rearrange("(c p) d -> p c d", p=128)[:, hf * 11:(hf + 1) * 11, dc * 128:(dc + 1) * 128]
                    dst_v = scr[f + "_d"][dc * 2 + hf].rearrange("p (c j) -> p c j", c=11)
                    cast_dma(dst_v, src_v, "%s_d%d" % (f, dc * 2 + hf))

        def cast_mix():
            for g in range(13):
                src_v = w_in_d.rearrange("(c p) f -> p c f", p=128)[:, :, g * 256:(g + 1) * 256]
                dst_v = scr["w_in"][g].rearrange("p (c j) -> p c j", c=8)
                cast_dma(dst_v, src_v, "w_in%d" % g)
            for g in range(4):
                src_v = w_out_d.rearrange("(c p) f -> p c f", p=128)[:, :, g * 256:(g + 1) * 256]
                dst_v = scr["w_out"][g].rearrange("p (c j) -> p c j", c=8)
                cast_dma(dst_v, src_v, "w_out%d" % g)

        cast_ffn("ffn1")
        if stage in ("mix", "full"):
            cast_mix()
        cast_ffn("ffn2")

        slot_i = [0]

        def wload(scr_ap, nm, ncols):
            k = slot_i[0] % NSLOT
            slot_i[0] += 1
            S.dma("sp", (lambda k, a, n: lambda e: e.dma_start(out=wslot[k][:, 0:n], in_=a))(k, scr_ap, ncols),
                  "ws%d" % k, reads=["scr_" + nm], writes=["wslot%d" % k])
            return k

        XC = lambda c: ["x%d_%d" % (c, b_) for b_ in range(4)]
        XB = lambda b_: ["x%d_%d" % (c_, b_) for c_ in range(NCH)]
        def finish_rstd(stat_bank, inv_n, eps_col=0, rows=None):
            S.op("act", lambda e: e.activation(out=rstd[:], in_=ps[stat_bank][:, :], func=AF.Ln,
                                               bias=epsT[:, eps_col:eps_col + 1], scale=inv_n),
                 reads=["ps%d" % stat_bank, "epsT"], writes=["rstd"])
            S.op("act", lambda e: e.activation(out=rstd[:], in_=rstd[:], func=AF.Exp, scale=-0.5),
                 reads=["rstd"], writes=["rstd"])

        def pre_norm(gname):
            for c in range(NCH):
                q = sqb[c % 2]
                qn = "sqb%d" % (c % 2)
                if c % 2 == 0:
                    S.op("dve", (lambda c, q: lambda e: e.tensor_tensor(out=q[:], in0=xT[:, c, :], in1=xT[:, c, :],
                                                                        op=ALU.mult))(c, q), reads=XC(c), writes=[qn])
                else:
                    S.op("act", (lambda c, q: lambda e: e.activation(out=q[:], in_=xT[:, c, :], func=AF.Square))(c, q),
                         reads=XC(c), writes=[qn])
                S.op("pe", (lambda c, q: lambda e: e.matmul(ps[6][:, :], lhsT=cB(CST_ONES), rhs=q[:],
                                                            start=(c == 0), stop=(c == NCH - 1)))(c, q),
                     reads=[qn, "cstb"], writes=["ps6"])
            finish_rstd(6, 1.0 / D)
            for c in range(NCH):
                S.op("dve", (lambda c: lambda e: e.scalar_tensor_tensor(
                    out=xnT[:, c, :], in0=xT[:, c, :], scalar=ppc(gname, c), in1=rstd[:],
                    op0=ALU.mult, op1=ALU.mult))(c), reads=XC(c) + ["rstd", "pp"], writes=["xnT%d" % c])

        def post_norm_residual(gi):
            finish_rstd(6, 1.0 / D)
            for c in range(NCH):
                S.op("dve", (lambda c: lambda e: e.scalar_tensor_tensor(
                    out=fT[:, c, :], in0=fT[:, c, :], scalar=ghalf[:, gi * 8 + c:gi * 8 + c + 1], in1=rstd[:],
                    op0=ALU.mult, op1=ALU.mult))(c), reads=["fT%d" % c, "rstd", "ghalf"], writes=["fT%d" % c])
                S.op("dve", (lambda c: lambda e: e.tensor_tensor(out=xT[:, c, :], in0=xT[:, c, :], in1=fT[:, c, :],
                                                                 op=ALU.add))(c),
                     reads=["fT%d" % c] + XC(c), writes=XC(c))

        def out_chunk_epilogue(dc, bd):
            S.op("dve", (lambda dc, bd: lambda e: e.tensor_copy(out=fT[:, dc, :], in_=ps[bd][:, :]))(dc, bd),
                 reads=["ps%d" % bd], writes=["fT%d" % dc])
            q = sqb[dc % 2]
            qn = "sqb%d" % (dc % 2)
            S.op("act", (lambda q, bd: lambda e: e.activation(out=q[:], in_=ps[bd][:, :], func=AF.Square))(q, bd),
                 reads=["ps%d" % bd], writes=[qn])

        def out_chunk_stats(dc):
            q = sqb[dc % 2]
            qn = "sqb%d" % (dc % 2)
            S.op("pe", (lambda dc, q: lambda e: e.matmul(ps[6][:, :], lhsT=cB(CST_ONES), rhs=q[:],
                                                         start=(dc == 0), stop=(dc == NCH - 1)))(dc, q),
                 reads=[qn, "cstb"], writes=["ps6"])

        def ffn(f, gpre, gi_post):
            pre_norm(gpre)
            xn_reads = ["xnT%d" % c for c in range(NCH)]
            for g in range(11):
                kg = wload(scr[f + "_g"][g], "%s_g%d" % (f, g), 2048)
                ku = wload(scr[f + "_u"][g], "%s_u%d" % (f, g), 2048)
                wg = wslot[kg][:, 0:2048].rearrange("p (c j) -> p c j", c=8)
                wu = wslot[ku][:, 0:2048].rearrange("p (c j) -> p c j", c=8)
                for j in range(2):
                    fc = 2 * g + j
                    bg, bu = fc % 2, 2 + fc % 2
                    for c in range(NCH):
                        S.op("pe", (lambda c, j, wg, bg: lambda e: e.matmul(
                            ps[bg][:, :], lhsT=wg[:, c, j * 128:(j + 1) * 128], rhs=xnT[:, c, :],
                            start=(c == 0), stop=(c == NCH - 1)))(c, j, wg, bg),
                            reads=["wslot%d" % kg] + xn_reads, writes=["ps%d" % bg])
                    for c in range(NCH):
                        S.op("pe", (lambda c, j, wu, bu: lambda e: e.matmul(
                            ps[bu][:, :], lhsT=wu[:, c, j * 128:(j + 1) * 128], rhs=xnT[:, c, :],
                            start=(c == 0), stop=(c == NCH - 1)))(c, j, wu, bu),
                            reads=["wslot%d" % ku] + xn_reads, writes=["ps%d" % bu])
                    sl = silb[fc % 2]
                    sn = "silb%d" % (fc % 2)
                    S.op("act", (lambda sl, bg: lambda e: e.activation(out=sl[:], in_=ps[bg][:, :], func=AF.Silu))(sl, bg),
                         reads=["ps%d" % bg], writes=[sn])
                    S.op("dve", (lambda sl, bu, fc: lambda e: e.tensor_tensor(
                        out=hT[:, fc, :], in0=sl[:], in1=ps[bu][:, :], op=ALU.mult))(sl, bu, fc),
                        reads=[sn, "ps%d" % bu], writes=["hT%d" % fc])
            h_reads = ["hT%d" % i for i in range(NFF)]
            for dc in range(NCH):
                bd = 4 + dc % 2
                for hf in range(2):
                    kd = wload(scr[f + "_d"][dc * 2 + hf], "%s_d%d" % (f, dc * 2 + hf), 1408)
                    wd = wslot[kd][:, 0:1408].rearrange("p (c j) -> p c j", c=11)
                    for i in range(11):
                        fc = hf * 11 + i
                        S.op("pe", (lambda fc, i, wd, bd: lambda e: e.matmul(
                            ps[bd][:, :], lhsT=wd[:, i, :], rhs=hT[:, fc, :],
                            start=(fc == 0), stop=(fc == NFF - 1)))(fc, i, wd, bd),
                            reads=["wslot%d" % kd] + h_reads, writes=["ps%d" % bd])
                out_chunk_epilogue(dc, bd)
                if dc >= 1:
                    out_chunk_stats(dc - 1)
            out_chunk_stats(NCH - 1)
            post_norm_residual(gi_post)

        def io_tile(Ts, Tl):
            for tb in range(4):
                if Tl < n_tiles:
                    r0 = Tl * TT + tb * 128
                    S.dma("sp", (lambda r0: lambda e: e.dma_start(out=xio[0][:], in_=x_d[r0:r0 + 128, :]))(r0),
                          "xio0", writes=["xio0"])
                if Ts >= 0:
                    r0 = Ts * TT + tb * 128
                    xo = xio[1]
                    for half in range(2):
                        for cc in range(4):
                            c = half * 4 + cc
                            S.op("pe", (lambda c, cc, tb: lambda e: e.transpose(
                                out=ps[5][:, cc * 128:(cc + 1) * 128], in_=xT[:, c, tb * 128:(tb + 1) * 128],
                                identity=cI(CST_IDENT)))(c, cc, tb),
                                reads=XB(tb) + ["cst"], writes=["ps5"])
                        S.op("act", (lambda half: lambda e: e.copy(out=xo[:, half * 512:(half + 1) * 512],
                                                                   in_=ps[5][:, :]))(half),
                             reads=["ps5"], writes=["xio1"])
                    S.dma("sp", (lambda r0: lambda e: e.dma_start(out=out_d[r0:r0 + 128, :], in_=xo[:]))(r0),
                          "outst", reads=["xio1"], writes=["out_hbm"])
                if Tl < n_tiles:
                    xi = xio[0]
                    for half in range(2):
                        for cc in range(4):
                            c = half * 4 + cc
                            S.op("pe", (lambda xi, c, cc: lambda e: e.transpose(
                                out=ps[7][:, cc * 128:(cc + 1) * 128], in_=xi[:, c * 128:(c + 1) * 128],
                                identity=cI(CST_IDENT)))(xi, c, cc),
                                reads=["xio0", "cst"], writes=["ps7"])
                        S.op("dve", (lambda half, tb: lambda e: e.tensor_copy(
                            out=xT[:, half * 4:half * 4 + 4, tb * 128:(tb + 1) * 128],
                            in_=ps[7][:, :].rearrange("p (c t) -> p c t", c=4)))(half, tb),
                            reads=["ps7"], writes=["x%d_%d" % (c_, tb) for c_ in range(half * 4, half * 4 + 4)])

        qT, vT = gbuf0, gbuf1
        vtok = xnT[:, 0:4, :]
        GB0 = ["xnT%d" % i for i in range(4)]
        xn_reads = ["xnT%d" % c for c in range(NCH)]

        def proj_fm(wt, kslot, j, bank):
            for c in range(NCH):
                S.op("pe", (lambda c: lambda e: e.matmul(
                    ps[bank][:, :], lhsT=wt[:, c, j * 128:(j + 1) * 128], rhs=xnT[:, c, :],
                    start=(c == 0), stop=(c == NCH - 1)))(c),
                    reads=["wslot%d" % kslot] + xn_reads, writes=["ps%d" % bank])

        def mixer_proj_sb(T):
            dx = mixT[:, :, :]
            dxn = ["mixT%d" % i for i in range(8)]
            S.op("dve", lambda e: e.tensor_tensor(out=dx[:, :, 1:TT], in0=xnT[:, :, 0:TT - 1], in1=xnT[:, :, 1:TT],
                                                  op=ALU.subtract), reads=xn_reads, writes=dxn)
            S.op("dve", lambda e: e.tensor_tensor(out=dx[:, :, 0:1], in0=xn_carry[:, :, 0:1], in1=xnT[:, :, 0:1],
                                                  op=ALU.subtract), reads=xn_reads + ["xn_carry"], writes=dxn)
            S.op("dve", lambda e: e.tensor_copy(out=xn_carry[:, :, 0:1], in_=xnT[:, :, TT - 1:TT]),
                 reads=xn_reads, writes=["xn_carry"])
            bank = [0]
            for gi in range(6):
                k = wload(scr["w_in"][gi], "w_in%d" % gi, 2048)
                wt = wslot[k][:, 0:2048].rearrange("p (c j) -> p c j", c=8)
                for j in range(2):
                    cc = (gi % 2) * 2 + j
                    b = bank[0] % 2
                    bank[0] += 1
                    proj_fm(wt, k, j, b)
                    if gi < 2:
                        S.op("act", (lambda cc, b: lambda e: e.mul(out=qT[:, cc, :], in_=ps[b][:, :], mul=0.125))(cc, b),
                             reads=["ps%d" % b], writes=["qT%d" % cc])
                    elif gi < 4:
                        S.op("act", (lambda cc, b: lambda e: e.copy(out=kTh[:, cc, T * TT:(T + 1) * TT],
                                                                    in_=ps[b][:, :]))(cc, b),
                             reads=["ps%d" % b], writes=["kT%d_%d" % (cc, T)])
                    else:
                        S.op("act", (lambda cc, b: lambda e: e.copy(out=vT[:, cc, :], in_=ps[b][:, :]))(cc, b),
                             reads=["ps%d" % b], writes=["vT%d" % cc])
                if gi >= 4:
                    for tb in range(4):
                        b = 2 + tb % 2
                        for c in range(NCH):
                            S.op("pe", (lambda c, tb, b, wt: lambda e: e.matmul(
                                ps[b][:, 0:256], lhsT=dx[:, c, tb * 128:(tb + 1) * 128], rhs=wt[:, c, :],
                                start=(c == 0), stop=(c == NCH - 1)))(c, tb, b, wt),
                                reads=["wslot%d" % k] + dxn, writes=["ps%d" % b])
                        S.op("dve", (lambda tb, b, gi: lambda e: e.tensor_copy(
                            out=dvh[:, 4 * T + tb, (gi - 4) * 256:(gi - 3) * 256], in_=ps[b][:, 0:256]))(tb, b, gi),
                            reads=["ps%d" % b], writes=["dv%d" % (4 * T + tb)])

        def sb_attention(T):
            nkb = 4 * T + 4
            w1, w2, w3 = wslot[1], wslot[2], wslot[3]
            spb = [w1[:, 0:512], w1[:, 512:1024]]
            Csum = w1[:, 1024:1536]
            Pb = [w1[:, 1536:2048], w2[:, 0:512]]
            sqA = w2[:, 512:1024]
            efb = [w2[:, 1024:2048].bitcast(F32), w3[:, 0:1024].bitcast(F32)]
            oT = w3[:, 1024:2048].bitcast(F32)
            WS_ = {"a_sp0": "wslot1", "a_sp1": "wslot1", "a_cs": "wslot1", "a_P0": "wslot1", "a_P1": "wslot2",
                   "a_sq": "wslot2", "a_ef0": "wslot2", "a_ef1": "wslot3", "a_oT": "wslot3"}

            def RD(names):
                return list(names) + sorted({WS_[n] for n in names if n in WS_})

            S.op("pool", lambda e: e.memset(Csum, 0.0), writes=["wslot1", "a_cs"])
            S.op("pool", lambda e: e.memset(sqA, 0.0), writes=["wslot2", "a_sq"])
            S.op("pool", lambda e: e.memset(oT, 0.0), writes=["wslot3", "a_oT"])
            items = []
            for c in range(4):
                for half in range(2):
                    for kb in reversed(range(nkb)):
                        items.append((c, half, kb))

            def geom(kb):
                diag = kb >= 4 * T
                n0 = (kb - 4 * T) * 128 if diag else 0
                return diag, n0, slice(n0, TT)

            def stage1(i):
                c, half, kb = items[i]
                hs = slice(64 * half, 64 * half + 64)
                diag, n0, cols = geom(kb)
                zb = 0
                ef, sp = efb[i % 2], spb[i % 2]
                efn, spn = "a_ef%d" % (i % 2), "a_sp%d" % (i % 2)
                S.op("pe", lambda e: e.matmul(
                    ps[zb][:, cols], lhsT=kTh[hs, c, kb * 128:(kb + 1) * 128], rhs=qT[hs, c, cols],
                    start=True, stop=True), reads=["kT%d_%d" % (c, kb // 4), "qT%d" % c], writes=["ps%d" % zb])
                S.op("act", lambda e: e.activation(out=ef[:, cols], in_=ps[zb][:, cols], func=AF.Exp),
                     reads=RD(["ps%d" % zb, efn]), writes=[efn])
                S.op("act", lambda e: e.activation(out=sp[:, cols], in_=ef[:, cols], func=AF.Ln,
                                                   bias=epsT[:, 3:4], scale=1.0),
                     reads=RD([efn, "epsT", spn]), writes=[spn])
                if diag:
                    S.op("dve", lambda e: e.tensor_tensor(out=sp[:, n0:n0 + 128], in0=sp[:, n0:n0 + 128],
                                                          in1=cB(CST_LT), op=ALU.mult),
                         reads=RD([spn, "cstb"]), writes=[spn])

            def stage2(i):
                c, half, kb = items[i]
                hs = slice(64 * half, 64 * half + 64)
                diag, n0, cols = geom(kb)
                first = kb == nkb - 1
                cb_ = 1
                ob = 2 + half
                sp, P = spb[i % 2], Pb[i % 2]
                spn, Pn = "a_sp%d" % (i % 2), "a_P%d" % (i % 2)
                if first:
                    S.op("pool", lambda e: e.memset(Csum, 0.0), reads=RD(["a_cs"]), writes=["a_cs"])
                S.op("pe", lambda e: e.matmul(ps[cb_][:, cols], lhsT=cB(CST_UI), rhs=sp[:, cols],
                                              start=True, stop=False), reads=RD([spn, "cstb"]), writes=["ps%d" % cb_])
                S.op("pe", lambda e: e.matmul(ps[cb_][:, cols], lhsT=cB(CST_ONES), rhs=Csum[:, cols],
                                              start=False, stop=True), reads=RD(["a_cs", "cstb"]), writes=["ps%d" % cb_])
                S.op("dve", lambda e: e.tensor_tensor(out=Csum[:, cols], in0=Csum[:, cols], in1=sp[:, cols],
                                                      op=ALU.add), reads=RD([spn, "a_cs"]), writes=["a_cs"])
                if first and n0 > 0:
                    S.op("pool", lambda e: e.memset(P[:, 0:n0], 0.0), reads=RD([Pn]), writes=[Pn])
                S.op("act", lambda e: e.activation(out=P[:, cols], in_=ps[cb_][:, cols], func=AF.Exp, scale=-1.0),
                     reads=RD(["ps%d" % cb_, Pn]), writes=[Pn])
                if diag:
                    S.op("dve", lambda e: e.tensor_tensor(out=P[:, n0:n0 + 128], in0=P[:, n0:n0 + 128],
                                                          in1=cB(CST_LE), op=ALU.mult),
                         reads=RD([Pn, "cstb"]), writes=[Pn])

            def stage3(i):
                c, half, kb = items[i]
                hs = slice(64 * half, 64 * half + 64)
                diag, n0, cols = geom(kb)
                first = kb == nkb - 1
                ob = 2 + half
                P = Pb[i % 2]
                Pn = "a_P%d" % (i % 2)
                pcols = slice(0, TT) if first else cols
                S.op("pe", lambda e: e.matmul(
                    ps[ob][:, pcols], lhsT=dvh[:, kb, c * 128:(c + 1) * 128], rhs=P[:, pcols],
                    start=first, stop=(kb == 0), skip_group_check=True),
                    reads=RD([Pn, "dv%d" % kb]), writes=["ps%d" % ob])
                if kb == 0:
                    S.op("dve", lambda e: e.tensor_tensor(out=oT[hs, :], in0=ps[ob][hs, :], in1=vT[hs, c, :],
                                                          op=ALU.add),
                         reads=RD(["ps%d" % ob, "vT%d" % c, "a_oT"]), writes=["a_oT"])
                    if half == 1:
                        head_norm(c)

            def head_norm(c):
                S.op("dve", lambda e: e.tensor_tensor(out=sqA, in0=oT, in1=oT, op=ALU.mult),
                     reads=RD(["a_oT", "a_sq"]), writes=["a_sq"])
                S.op("pe", lambda e: e.matmul(ps[1][:, :], lhsT=cB(CST_BONES), rhs=sqA, start=True, stop=True),
                     reads=RD(["a_sq", "cstb"]), writes=["ps1"])
                finish_rstd(1, 1.0 / 64)
                S.op("dve", lambda e: e.scalar_tensor_tensor(
                    out=mixT[:, c, :], in0=oT, scalar=ppc("sb_out_g", c), in1=rstd[:],
                    op0=ALU.mult, op1=ALU.mult), reads=RD(["a_oT", "rstd", "pp"]), writes=["mixT%d" % c])

            n = len(items)
            for s_ in range(n + 2):
                if s_ < n:
                    stage1(s_)
                if 1 <= s_ <= n:
                    stage2(s_ - 1)
                if s_ >= 2:
                    stage3(s_ - 2)

        def rwkv_prep(T):
            order = [12, 6, 7, 8, 9, 10, 11]
            bank = [0]
            r_b = lambda c: hTc(c)
            k2_b = lambda c: hTc(4 + c)
            kkn_b = lambda c: hTc(8 + c)
            b_b = lambda c: hTc(12 + c)
            vr_b = lambda c: hTc(16 + c)
            twa, sgT = hTc(20), hTc(21)
            chunks = [(gi, j) for gi in order for j in range(2)]
            slots = {}

            def emit_proj(i):
                gi, j = chunks[i]
                if j == 0:
                    k = wload(scr["w_in"][gi], "w_in%d" % gi, 2048)
                    slots[gi] = (k, wslot[k][:, 0:2048].rearrange("p (c j) -> p c j", c=8))
                k, wt = slots[gi]
                proj_fm(wt, k, j, 4 + i % 2)

            def post_chunk(ch, b):
                S.op("act", (lambda b: lambda e: e.copy(out=pst[:, 1:TT + 1], in_=ps[b][:, :]))(b),
                     reads=["ps%d" % b], writes=["pst"])
                S.op("pool", (lambda ch: lambda e: e.tensor_copy(out=pst[:, 0:1], in_=prw_carry[:, ch:ch + 1]))(ch),
                     reads=["prw_carry"], writes=["pst"])
                pm = fTc(0)
                S.op("dve", (lambda ch: lambda e: e.tensor_scalar(
                    out=fTc(1), in0=pst[:, 0:TT], scalar1=ppc("shift_mu", ch), scalar2=None, op0=ALU.mult))(ch),
                    reads=["pst", "pp"], writes=["fT1"])
                if ch < 4:
                    dst, dn = r_b(ch), "hT%d" % ch
                elif ch < 8:
                    dst, dn = fTc(2), "fT2"
                elif ch < 12:
                    dst, dn = vr_b(ch - 8), "hT%d" % (16 + ch - 8)
                else:
                    dst, dn = pm, "fT0"
                S.op("dve", (lambda ch, dst: lambda e: e.scalar_tensor_tensor(
                    out=dst, in0=pst[:, 1:TT + 1], scalar=onem[:, ch:ch + 1], in1=fTc(1),
                    op0=ALU.mult, op1=ALU.add))(ch, dst), reads=["pst", "onem", "fT1"], writes=[dn])
                S.op("pool", (lambda ch: lambda e: e.tensor_copy(out=prw_carry[:, ch:ch + 1], in_=pst[:, TT:TT + 1]))(ch),
                     reads=["pst"], writes=["prw_carry"])
                if ch == 12:
                    S.op("act", lambda e: e.activation(out=twa[0:64, :], in_=pm[0:64, :], func=AF.Tanh),
                         reads=["fT0"], writes=["hT20"])
                    S.op("act", lambda e: e.copy(out=twa[64:128, :], in_=pm[64:128, :]),
                         reads=["fT0"], writes=["hT20"])
                elif ch == 13:
                    S.op("act", lambda e: e.activation(out=sgT, in_=pm, func=AF.Sigmoid),
                         reads=["fT0"], writes=["hT21"])
                elif 4 <= ch < 8:
                    c = ch - 4
                    kr = fTc(2)
                    a_f, kk, t3, t4 = fTc(3), fTc(4), fTc(5), fTc(6)
                    S.op("pe", (lambda c: lambda e: e.matmul(
                        ps[6][:, :], lhsT=lora_w[64:128, c * 128:(c + 1) * 128], rhs=twa[64:128, :],
                        start=True, stop=True))(c), reads=["lora_w", "hT20"], writes=["ps6"])
                    S.op("act", (lambda c: lambda e: e.activation(out=a_f, in_=ps[6][:, :], func=AF.Sigmoid,
                                                                  bias=ppc("iclr_a0", c), scale=1.0))(c),
                         reads=["ps6", "pp"], writes=["fT3"])
                    S.op("dve", (lambda c: lambda e: e.tensor_scalar(out=kk, in0=kr, scalar1=ppc("k_k", c),
                                                                     scalar2=None, op0=ALU.mult))(c),
                         reads=["fT2", "pp"], writes=["fT4"])
                    S.op("dve", lambda e: e.tensor_tensor(out=t3, in0=kk, in1=kk, op=ALU.mult),
                         reads=["fT4"], writes=["fT5"])
                    S.op("pe", lambda e: e.matmul(ps[7][:, :], lhsT=cI(CST_BONES), rhs=t3, start=True, stop=True),
                         reads=["fT5", "cst"], writes=["ps7"])
                    S.op("dve", lambda e: e.tensor_scalar(out=t3, in0=ps[7][:, :], scalar1=1e-24, scalar2=None,
                                                          op0=ALU.max), reads=["ps7"], writes=["fT5"])
                    S.op("act", lambda e: e.activation(out=t3, in_=t3, func=AF.Ln), reads=["fT5"], writes=["fT5"])
                    S.op("act", lambda e: e.activation(out=t3, in_=t3, func=AF.Exp, scale=-0.5),
                         reads=["fT5"], writes=["fT5"])
                    S.op("dve", (lambda c: lambda e: e.tensor_tensor(out=kkn_b(c), in0=kk, in1=t3, op=ALU.mult))(c),
                         reads=["fT4", "fT5"], writes=["hT%d" % (8 + c)])
                    S.op("dve", (lambda c: lambda e: e.tensor_scalar(out=t4, in0=a_f, scalar1=-1.0,
                                                                     scalar2=ppc("k_a", c), op0=ALU.add,
                                                                     op1=ALU.mult))(c),
                         reads=["fT3", "pp"], writes=["fT6"])
                    S.op("dve", (lambda c: lambda e: e.scalar_tensor_tensor(
                        out=k2_b(c), in0=t4, scalar=1.0, in1=kr, op0=ALU.add, op1=ALU.mult))(c),
                        reads=["fT6", "fT2"], writes=["hT%d" % (4 + c)])
                    S.op("dve", (lambda c: lambda e: e.tensor_tensor(out=b_b(c), in0=kkn_b(c), in1=a_f,
                                                                     op=ALU.mult))(c),
                         reads=["hT%d" % (8 + c), "fT3"], writes=["hT%d" % (12 + c)])
                    S.op("dve", (lambda c: lambda e: e.scalar_tensor_tensor(
                        out=silb[0][:], in0=r_b(c), scalar=ppc("r_k", c), in1=k2_b(c),
                        op0=ALU.mult, op1=ALU.mult))(c),
                        reads=["hT%d" % c, "hT%d" % (4 + c), "pp"], writes=["silb0"])
                    hsel = cstb[:, CST_BONES * 128:(CST_BONES + 1) * 128:64]
                    for tb in range(4):
                        S.op("pe", (lambda tb: lambda e: e.matmul(
                            ps[7][:, tb * 2:tb * 2 + 2], lhsT=silb[0][:, tb * 128:(tb + 1) * 128], rhs=hsel,
                            start=True, stop=True))(tb), reads=["silb0", "cstb"], writes=["ps7"])
                    S.op("dve", (lambda c: lambda e: e.tensor_copy(
                        out=bonus[:, :, 2 * c:2 * c + 2],
                        in_=ps[7][:, 0:8].rearrange("p (t j) -> p t j", t=4)))(c),
                        reads=["ps7"], writes=["bonus"])

            emit_proj(0)
            for i in range(len(chunks)):
                if i + 1 < len(chunks):
                    emit_proj(i + 1)
                gi, j = chunks[i]
                post_chunk((gi - 6) * 2 + j, 4 + i % 2)

        def rwkv_prep_tail():
            vr_b = lambda c: hTc(16 + c)
            for c in range(4):
                for tb in range(4):
                    S.op("pe", (lambda c, tb: lambda e: e.transpose(
                        out=psb[7][:, tb * 128:(tb + 1) * 128], in_=vr_b(c)[:, tb * 128:(tb + 1) * 128],
                        identity=cB(CST_IDENT)))(c, tb),
                        reads=["hT%d" % (16 + c), "cstb"], writes=["ps7"])
                S.op("dve", (lambda c: lambda e: e.tensor_copy(
                    out=vtok[:, :, c * 128:(c + 1) * 128],
                    in_=psb[7][:, 0:512].rearrange("p (t j) -> p t j", t=4)))(c),
                    reads=["ps7"], writes=GB0)


        def rwkv_chunk(T, tb):
            tsl = slice(tb * 128, (tb + 1) * 128)
            r4 = hT[:, 0:4, tsl]
            k4 = hT[:, 4:8, tsl]
            kkn4 = hT[:, 8:12, tsl]
            b4 = hT[:, 12:16, tsl]
            rn_, kn_, kknn_, bn_ = (["hT%d" % (o + i) for i in range(4)] for o in (0, 4, 8, 12))
            twa, sgT = hTc(20), hTc(21)
            V4 = lambda t: t.rearrange("p (c j) -> p c j", c=4)
            kt, bt = V4(hT[:, 17, :]), V4(hT[:, 18, :])
            rt_bd = fT[:, 7, :].bitcast(BF16).rearrange("p (c j) -> p c j", c=4)
            at_bd = fT[:, 6, :].bitcast(BF16).rearrange("p (c j) -> p c j", c=4)
            rtn, ktn, btn, atn = "fT7", "hT17", "hT18", "fT6"
            HA, HB = slice(0, 64), slice(64, 128)
            Kh, Bh = V4(xnT[:, 4, :]), V4(xnT[:, 5, :])
            Kht, Bht = xnT[:, 6, :], xnT[:, 7, :]
            lw, tsp = fTc(0), fTc(1)
            Et, y = fTc(2), fTc(3)
            S.op("pe", lambda e: e.matmul(ps[4][:, :], lhsT=twa[0:64, tsl], rhs=lora_w[0:64, :], start=True, stop=True),
                 reads=["hT20", "lora_w"], writes=["ps4"])
            S.op("dve", lambda e: e.tensor_tensor(out=lw, in0=ps[4][:, :], in1=w0b[:], op=ALU.add),
                 reads=["ps4", "w0b"], writes=["fT0"])
            S.op("act", lambda e: e.activation(out=tsp, in_=lw, func=AF.Exp, scale=-1.0), reads=["fT0"], writes=["fT1"])
            S.op("act", lambda e: e.activation(out=tsp, in_=tsp, func=AF.Ln, bias=epsT[:, 3:4], scale=1.0),
                 reads=["fT1", "epsT"], writes=["fT1"])
            S.op("act", lambda e: e.activation(out=lw, in_=tsp, func=AF.Exp, bias=epsT[:, 2:3], scale=-1.0),
                 reads=["fT1", "epsT"], writes=["fT0"])
            LTLE = cst[:, CST_LT * 128:(CST_LE + 1) * 128]
            for c in range(4):
                S.op("pe", (lambda c: lambda e: e.matmul(
                    ps[4 + c // 2][:, (c % 2) * 256:(c % 2) * 256 + 256], lhsT=lw[:, c * 128:(c + 1) * 128], rhs=LTLE,
                    start=True, stop=True))(c), reads=["fT0", "cst"], writes=["ps%d" % (4 + c // 2)])
            G = lambda b: ps[4 + b][:, :].rearrange("p (c t) -> p c t", c=2)
            Lx = lambda b: G(b)[:, :, 0:128]
            Lg = lambda b: G(b)[:, :, 128:256]
            E = [V4(tmpf[0][:]), V4(tmpf[1][:])]
            for b in range(2):
                S.op("dve", (lambda b: lambda e: e.tensor_scalar(
                    out=nLgC[:, 2 * b:2 * b + 2], in0=G(b)[:, :, 255], scalar1=-1.0, scalar2=None, op0=ALU.mult))(b),
                    reads=["ps%d" % (4 + b)], writes=["nLgC"])
            S.op("act", lambda e: e.activation(out=gamC[:], in_=nLgC[:], func=AF.Exp), reads=["nLgC"], writes=["gamC"])

            def expE(i, src, scale, nm):
                for b in range(2):
                    S.op("act", (lambda b: lambda e: e.activation(out=E[i][:, 2 * b:2 * b + 2, :], in_=src(b),
                                                                  func=AF.Exp, scale=scale))(b),
                         reads=["ps%d" % (4 + b)], writes=[nm])

            expE(0, Lg, -1.0, "tmpf0")
            S.op("dve", lambda e: e.tensor_tensor(out=rt_bd[HA, :, 0:128], in0=r4[HA], in1=E[0][HA], op=ALU.mult),
                 reads=rn_ + ["tmpf0"], writes=[rtn])
            S.op("dve", lambda e: e.tensor_tensor(out=rt_bd[HB, :, 128:256], in0=r4[HB], in1=E[0][HB], op=ALU.mult),
                 reads=rn_ + ["tmpf0"], writes=[rtn])
            expE(1, Lg, 1.0, "tmpf1")
            S.op("dve", lambda e: e.tensor_tensor(out=kt, in0=k4, in1=E[1], op=ALU.mult),
                 reads=kn_ + ["tmpf1"], writes=[ktn])
            S.op("dve", lambda e: e.tensor_tensor(out=bt, in0=b4, in1=E[1], op=ALU.mult),
                 reads=bn_ + ["tmpf1"], writes=[btn])
            expE(0, Lx, -1.0, "tmpf0")
            S.op("dve", lambda e: e.scalar_tensor_tensor(out=at_bd[HA, :, 0:128], in0=kkn4[HA], scalar=-1.0,
                                                         in1=E[0][HA], op0=ALU.mult, op1=ALU.mult),
                 reads=kknn_ + ["tmpf0"], writes=[atn])
            S.op("dve", lambda e: e.scalar_tensor_tensor(out=at_bd[HB, :, 128:256], in0=kkn4[HB], scalar=-1.0,
                                                         in1=E[0][HB], op0=ALU.mult, op1=ALU.mult),
                 reads=kknn_ + ["tmpf0"], writes=[atn])
            for c in range(4):
                S.op("act", (lambda c: lambda e: e.activation(
                    out=E[1][:, c, :], in_=G(c // 2)[:, c % 2, 128:256], func=AF.Exp,
                    bias=nLgC[:, c:c + 1], scale=1.0))(c), reads=["ps%d" % (4 + c // 2), "nLgC"], writes=["tmpf1"])
            S.op("dve", lambda e: e.tensor_tensor(out=Kh, in0=k4, in1=E[1], op=ALU.mult),
                 reads=kn_ + ["tmpf1"], writes=["xnT4"])
            S.op("dve", lambda e: e.tensor_tensor(out=Bh, in0=b4, in1=E[1], op=ALU.mult),
                 reads=bn_ + ["tmpf1"], writes=["xnT5"])
            for src, sn, dst, dn in ((Kh, "xnT4", Kht, "xnT6"), (Bh, "xnT5", Bht, "xnT7")):
                for c in range(4):
                    S.op("pe", (lambda c, src: lambda e: e.transpose(
                        out=psb[7][:, c * 128:(c + 1) * 128], in_=src[:, c, :], identity=cB(CST_IDENT)))(c, src),
                        reads=[sn, "cstb"], writes=["ps7"])
                S.op("act", (lambda dst: lambda e: e.copy(out=dst, in_=psb[7][:, 0:512]))(dst),
                     reads=["ps7"], writes=[dn])
            mLT = cB(CST_LT).unsqueeze(1).broadcast_to([128, 4, 128])
            mLE = cB(CST_LE).unsqueeze(1).broadcast_to([128, 4, 128])
            mGT = cB(CST_GT).unsqueeze(1).broadcast_to([128, 4, 128])
            mID = cB(CST_IDENT).unsqueeze(1).broadcast_to([128, 4, 128])
            P4 = lambda b: ps[b][:, :].rearrange("p (h t) -> p h t", h=4)
            def half_gen(h4):
                A, B, Mk = halfA[h4], halfB[h4], halfK[h4]
                An, Bn, Mkn = "halfA%d" % h4, "halfB%d" % h4, "halfK%d" % h4
                Tf = Tfin[:, 4 * h4:4 * h4 + 4, :]
                Tn = "Tfin%d" % h4
                pb = [2 + 3 * h4 - (0 if h4 == 0 else 1) + i for i in range(3)]
                pb = [4, 5, 6]

                def mm_pair(bank, lf, rbd, rd):
                    for cp in range(2):
                        c = 2 * h4 + cp
                        S.op("pe", (lambda cp, c: lambda e: e.matmul(
                            ps[bank][:, cp * 256:(cp + 1) * 256], lhsT=lf[:, c, :], rhs=rbd[:, c, :],
                            start=True, stop=True))(cp, c), reads=rd, writes=["ps%d" % bank])

                def mm_pairT(bank, lbd, rf, rd):
                    for i in range(4):
                        c, hf = 2 * h4 + i // 2, i % 2
                        S.op("pe", (lambda i, c, hf: lambda e: e.matmul(
                            ps[bank][:, i * 128:(i + 1) * 128], lhsT=lbd[:, c, hf * 128:(hf + 1) * 128], rhs=rf[:, c, :],
                            start=True, stop=True))(i, c, hf), reads=rd, writes=["ps%d" % bank])

                mm_pair(pb[0], bt, at_bd, [btn, atn])
                S.op("dve", (lambda A, b0: lambda e: e.tensor_tensor(out=A[:], in0=P4(b0), in1=mLT, op=ALU.mult))(A, pb[0]),
                     reads=["ps%d" % pb[0], "cstb"], writes=[An])
                yield
                mm_pairT(pb[1], at_bd, bt, [btn, atn])
                S.op("dve", (lambda B, b1: lambda e: e.tensor_tensor(out=B[:], in0=P4(b1), in1=mGT, op=ALU.mult))(B, pb[1]),
                     reads=["ps%d" % pb[1], "cstb"], writes=[Bn])
                yield
                mm_pair(pb[2], kt, at_bd, [ktn, atn])
                S.op("dve", (lambda Mk, b2: lambda e: e.tensor_tensor(out=Mk[:], in0=P4(b2), in1=mLT, op=ALU.mult))(Mk, pb[2]),
                     reads=["ps%d" % pb[2], "cstb"], writes=[Mkn])
                yield
                mm_pair(pb[0], bt, rt_bd, [btn, rtn])
                S.op("dve", (lambda b0, h4: lambda e: e.tensor_tensor(out=Mbr[:, 4 * h4:4 * h4 + 4, :], in0=P4(b0), in1=mLE,
                                                                      op=ALU.mult))(pb[0], h4),
                     reads=["ps%d" % pb[0], "cstb"], writes=["Mbr%d" % h4])
                yield
                mm_pair(pb[1], kt, rt_bd, [ktn, rtn])
                S.op("dve", (lambda b1, h4: lambda e: e.tensor_tensor(out=Mkr[:, 4 * h4:4 * h4 + 4, :], in0=P4(b1), in1=mLE,
                                                                      op=ALU.mult))(pb[1], h4),
                     reads=["ps%d" % pb[1], "cstb"], writes=["Mkr%d" % h4])
                yield
                for i in range(4):
                    h = 4 * h4 + i
                    S.op("pe", (lambda i, h, Mk: lambda e: e.matmul(
                        ps[7][:, h * 64:(h + 1) * 64], lhsT=Mk[:, i, :], rhs=vtok[:, tb, h * 64:(h + 1) * 64],
                        start=True, stop=True))(i, h, Mk), reads=[Mkn] + GB0, writes=["ps7"])
                S.op("dve", (lambda A, Tf: lambda e: e.tensor_tensor(out=Tf, in0=A[:], in1=mID, op=ALU.add))(A, Tf),
                     reads=[An, "cstb"], writes=[Tn])
                for lvl in range(1, 7):
                    if lvl <= 5:
                        for i in range(4):
                            S.op("pe", (lambda i, A, B, b0: lambda e: e.matmul(
                                ps[b0][:, i * 128:(i + 1) * 128], lhsT=B[:, i, :], rhs=A[:, i, :],
                                start=True, stop=True))(i, A, B, pb[0]), reads=[An, Bn], writes=["ps%d" % pb[0]])
                    for i in range(4):
                        S.op("pe", (lambda i, A, B, b1: lambda e: e.matmul(
                            ps[b1][:, i * 128:(i + 1) * 128], lhsT=A[:, i, :], rhs=B[:, i, :],
                            start=True, stop=True))(i, A, B, pb[1]), reads=[An, Bn], writes=["ps%d" % pb[1]])
                    if lvl <= 5:
                        if T >= 4:
                            S.op("dve", (lambda A, b0: lambda e: e.tensor_copy(out=A[:], in_=P4(b0)))(A, pb[0]),
                                 reads=["ps%d" % pb[0]], writes=[An])
                        else:
                            S.op("act", (lambda A, b0: lambda e: e.copy(out=A[:], in_=P4(b0)))(A, pb[0]),
                                 reads=["ps%d" % pb[0]], writes=[An])
                    S.op("dve", (lambda B, b1: lambda e: e.tensor_copy(out=B[:], in_=P4(b1)))(B, pb[1]),
                         reads=["ps%d" % pb[1]], writes=[Bn])
                    for i in range(4):
                        S.op("pe", (lambda i, B, Tf, b2: lambda e: e.matmul(
                            ps[b2][:, i * 128:(i + 1) * 128], lhsT=B[:, i, :], rhs=Tf[:, i, :],
                            start=True, stop=True))(i, B, Tf, pb[2]), reads=[Bn, Tn], writes=["ps%d" % pb[2]])
                    S.op("dve", (lambda Tf, b2: lambda e: e.tensor_tensor(out=Tf, in0=Tf, in1=P4(b2), op=ALU.add))(Tf, pb[2]),
                         reads=["ps%d" % pb[2], Tn], writes=[Tn])
                    yield
            gens = [half_gen(0), half_gen(1)]
            while gens:
                for g_ in list(gens):
                    try:
                        next(g_)
                    except StopIteration:
                        gens.remove(g_)
            S.op("act", lambda e: e.copy(out=Et, in_=ps[7][:, :]), reads=["ps7"], writes=["fT2"])
            Xt, Ut = silb[0], silb[1]
            HS = lambda h: slice(64 * (h % 2), 64 * (h % 2) + 64)
            for h in range(8):
                S.op("pe", (lambda h: lambda e: e.matmul(
                    ps[4][:, h * 64:(h + 1) * 64], lhsT=at_bd[:, h // 2, (h % 2) * 128:(h % 2) * 128 + 128],
                    rhs=Sb[:, h // 2, :], start=True, stop=True))(h), reads=[atn, "Sb"], writes=["ps4"])
            S.op("dve", lambda e: e.tensor_tensor(out=Xt[:], in0=ps[4][:, :], in1=Et, op=ALU.add),
                 reads=["ps4", "fT2"], writes=["silb0"])
            for h in range(8):
                S.op("pe", (lambda h: lambda e: e.matmul(
                    ps[5][:, h * 64:(h + 1) * 64], lhsT=Tfin[:, h, :], rhs=Xt[:, h * 64:(h + 1) * 64],
                    start=True, stop=True))(h), reads=["Tfin%d" % (h // 4), "silb0"], writes=["ps5"])
            S.op("act", lambda e: e.copy(out=Ut[:], in_=ps[5][:, :]), reads=["ps5"], writes=["silb1"])
            for h in range(8):
                hc = slice(h * 64, (h + 1) * 64)
                S.op("pe", (lambda h, hc: lambda e: e.matmul(
                    ps[6][:, hc], lhsT=rt_bd[:, h // 2, (h % 2) * 128:(h % 2) * 128 + 128], rhs=Sb[:, h // 2, :],
                    start=True, stop=False))(h, hc),
                    reads=[rtn, "Sb"], writes=["ps6"])
                S.op("pe", (lambda h, hc: lambda e: e.matmul(
                    ps[6][:, hc], lhsT=Mbr[:, h, :], rhs=Ut[:, hc], start=False, stop=False))(h, hc),
                    reads=["Mbr%d" % (h // 4), "silb1"], writes=["ps6"])
                S.op("pe", (lambda h, hc: lambda e: e.matmul(
                    ps[6][:, hc], lhsT=Mkr[:, h, :], rhs=vtok[:, tb, hc], start=False, stop=True))(h, hc),
                    reads=["Mkr%d" % (h // 4)] + GB0, writes=["ps6"])
            S.op("act", lambda e: e.copy(out=y, in_=ps[6][:, :]), reads=["ps6"], writes=["fT3"])
            for h in range(8):
                hc = slice(h * 64, (h + 1) * 64)
                pc = slice((h // 2) * 128, (h // 2) * 128 + 128)
                S.op("pe", (lambda hc, pc: lambda e: e.matmul(
                    ps[7][:, hc], lhsT=Bht[:, pc], rhs=Ut[:, hc], start=True, stop=False))(hc, pc),
                    reads=["xnT7", "silb1"], writes=["ps7"])
                S.op("pe", (lambda hc, pc: lambda e: e.matmul(
                    ps[7][:, hc], lhsT=Kht[:, pc], rhs=vtok[:, tb, hc], start=False, stop=True))(hc, pc),
                    reads=["xnT6"] + GB0, writes=["ps7"])
            S.op("dve", lambda e: e.tensor_tensor(out=Sf[:], in0=Sf[:], in1=gamC[:, :].unsqueeze(2).broadcast_to([128, 4, 64]),
                                                  op=ALU.mult), reads=["Sf", "gamC"], writes=["Sf"])
            for hf, rs in ((0, slice(0, 64)), (1, slice(64, 128))):
                S.op("dve", (lambda hf, rs: lambda e: e.tensor_tensor(
                    out=Sf[rs], in0=Sf[rs],
                    in1=ps[7][rs, :].rearrange("p (c h v) -> p c h v", c=4, h=2)[:, :, hf, :], op=ALU.add))(hf, rs),
                    reads=["Sf", "ps7"], writes=["Sf"])
            S.op("dve", lambda e: e.tensor_copy(out=Sb[:], in_=Sf[:]), reads=["Sf"], writes=["Sb"])
            y3 = y.rearrange("p (h j) -> p h j", h=8)
            yc = fTc(4).rearrange("p (h j) -> p h j", h=8)
            ysq = fTc(5).rearrange("p (h j) -> p h j", h=8)
            s1, s2 = gst[:, 0, :], gst[:, 1, :]
            bc8 = lambda a: a.unsqueeze(2).broadcast_to([128, 8, 64])
            S.op("dve", lambda e: e.tensor_reduce(out=s1, in_=y3, axis=AX.X, op=ALU.add), reads=["fT3"], writes=["gst"])
            S.op("dve", lambda e: e.tensor_scalar(out=s1, in0=s1, scalar1=1.0 / 64, scalar2=None, op0=ALU.mult),
                 reads=["gst"], writes=["gst"])
            S.op("dve", lambda e: e.tensor_tensor(out=yc, in0=y3, in1=bc8(s1), op=ALU.subtract),
                 reads=["fT3", "gst"], writes=["fT4"])
            S.op("dve", lambda e: e.tensor_tensor(out=ysq, in0=yc, in1=yc, op=ALU.mult), reads=["fT4"], writes=["fT5"])
            S.op("dve", lambda e: e.tensor_reduce(out=s2, in_=ysq, axis=AX.X, op=ALU.add), reads=["fT5"], writes=["gst"])
            S.op("act", lambda e: e.activation(out=s2, in_=s2, func=AF.Ln, bias=epsT[:, 1:2], scale=1.0 / 64),
                 reads=["gst", "epsT"], writes=["gst"])
            S.op("act", lambda e: e.activation(out=s2, in_=s2, func=AF.Exp, scale=-0.5), reads=["gst"], writes=["gst"])
            S.op("dve", lambda e: e.tensor_tensor(out=yc, in0=yc, in1=bc8(s2), op=ALU.mult),
                 reads=["fT4", "gst"], writes=["fT4"])
            S.op("dve", lambda e: e.tensor_tensor(out=fTc(4), in0=fTc(4), in1=gng[:], op=ALU.mult),
                 reads=["fT4", "gng"], writes=["fT4"])
            S.op("dve", lambda e: e.tensor_tensor(out=fTc(4), in0=fTc(4), in1=gnb[:], op=ALU.add),
                 reads=["fT4", "gnb"], writes=["fT4"])
            S.op("dve", lambda e: e.tensor_tensor(out=ysq, in0=vtok[:, tb, :].rearrange("p (h j) -> p h j", h=8),
                                                  in1=bc8(bonus[:, tb, :]), op=ALU.mult),
                 reads=GB0 + ["bonus"], writes=["fT5"])
            S.op("dve", lambda e: e.tensor_tensor(out=fTc(4), in0=fTc(4), in1=fTc(5), op=ALU.add),
                 reads=["fT4", "fT5"], writes=["fT4"])
            S.op("pe", lambda e: e.matmul(ps[4][:, :], lhsT=sgT[:, tsl], rhs=gate_wb[:], start=True, stop=True),
                 reads=["hT21", "gate_wb"], writes=["ps4"])
            yfin = sqb[0]
            S.op("dve", lambda e: e.tensor_tensor(out=yfin[:], in0=fTc(4), in1=ps[4][:, :], op=ALU.mult),
                 reads=["fT4", "ps4"], writes=["sqb0"])
            for c in range(4):
                S.op("pe", (lambda c: lambda e: e.transpose(
                    out=psb[7][:, c * 128:(c + 1) * 128], in_=yfin[:, c * 128:(c + 1) * 128], identity=cB(CST_IDENT)))(c),
                    reads=["sqb0", "cstb"], writes=["ps7"])
            S.op("act", lambda e: e.copy(out=mixT[:, 4:8, tsl], in_=psb[7][:, 0:512].rearrange("p (c t) -> p c t", c=4)),
                 reads=["ps7"], writes=["mixT%d" % (4 + i) for i in range(4)])

        def mixer_out(T):
            m_reads = ["mixT%d" % i for i in range(8)]
            for gi in range(4):
                k = wload(scr["w_out"][gi], "w_out%d" % gi, 2048)
                wt = wslot[k][:, 0:2048].rearrange("p (c j) -> p c j", c=8)
                for j in range(2):
                    dc = gi * 2 + j
                    bd = 4 + dc % 2
                    for c in range(NCH):
                        S.op("pe", (lambda c, j, wt, bd: lambda e: e.matmul(
                            ps[bd][:, :], lhsT=wt[:, c, j * 128:(j + 1) * 128], rhs=mixT[:, c, :],
                            start=(c == 0), stop=(c == NCH - 1)))(c, j, wt, bd),
                            reads=["wslot%d" % k] + m_reads, writes=["ps%d" % bd])
                    out_chunk_epilogue(dc, bd)
                    if dc >= 1:
                        out_chunk_stats(dc - 1)
            out_chunk_stats(NCH - 1)
            post_norm_residual(1)

        def mixer(T):
            pre_norm("mix_pre_g")
            S.begin_record()
            mixer_proj_sb(T)
            rec_p = S.end_record()
            S.begin_record()
            rwkv_prep(T)
            rec_q = S.end_record()
            S.replay_merged(rec_p, rec_q, cost_a={"pe": 0.4, "act": 0.6, "dve": 0.5, "pool": 0.5},
                            cost_b={"pe": 0.4, "act": 0.6, "dve": 0.65, "pool": 0.3})
            rwkv_prep_tail()
            S.begin_record()
            sb_attention(T)
            rec_a = S.end_record()
            S.begin_record()
            S.op("pool", lambda e: e.memset(fT[:, 6:8, :], 0.0), writes=["fT6", "fT7"])
            for tb in range(4):
                rwkv_chunk(T, tb)
            rec_b = S.end_record()
            S.replay_merged(rec_a, rec_b, cost_a={"pe": 0.45, "act": 0.6, "dve": 0.45, "pool": 0.5},
                            cost_b={"pe": 0.13, "act": 0.55, "dve": 0.65, "pool": 0.9})
            if dbg:
                S.dma("sp", (lambda T: lambda e: e.dma_start(out=dbg_d[T], in_=mixT[:].rearrange("p c t -> p (c t)")))(T),
                      "dbg", reads=["mixT%d" % i for i in range(8)], writes=["dbg_hbm"])
            mixer_out(T)

        io_tile(-1, 0)
        for T in range(n_tiles):
            if stage in ("ffn1", "ffn12", "full"):
                ffn("ffn1", "ffn1_pre_g", 0)
            if stage in ("mix", "full"):
                mixer(T)
            if stage in ("ffn12", "full"):
                ffn("ffn2", "ffn2_pre_g", 2)
            io_tile(T, T + 1)
        S.final_wait("sp", ["out_hbm"] + (["dbg_hbm"] if dbg else []))
        S.emit(nc, st)
    return nc


def host_inputs(inputs, b):
    m = {"x": np.ascontiguousarray(inputs["x"][b]), "cst": make_consts()}
    ppa = np.zeros((128, NPP), np.float32)
    for name, (o, n) in PP.items():
        v = np.asarray(inputs[name], np.float32).reshape(-1)
        ppa[:, o:o + n] = v.reshape(n, 128).T
    m["pp"] = ppa
    for f in ("ffn1", "ffn2"):
        for w in ("_w_gate", "_w_up", "_w_down"):
            m[f + w] = np.ascontiguousarray(np.asarray(inputs[f + w], np.float32)[0])
    for nm in ("w_in", "w_out", "decay_w2", "iclr_a2", "gate_w2"):
        m[nm] = np.ascontiguousarray(np.asarray(inputs[nm], np.float32)[0])
    for nm in ("decay_w0", "gn_g", "gn_b"):
        m[nm] = np.ascontiguousarray(np.asarray(inputs[nm], np.float32).reshape(1, 512))
    return m


def kernel(**inputs):
    inputs = {k: np.asarray(v) for k, v in inputs.items()}
    nc = build_nc()
    in_maps = [host_inputs(inputs, b) for b in range(8)]
    res = run_bass_kernel_spmd(nc, in_maps, core_ids=list(range(8)))
    return np.stack([np.asarray(r["out"], np.float32) for r in res.results], axis=0)
```

```python
import numpy as np
from contextlib import ExitStack
import concourse.bass as bass
import concourse.mybir as mybir
from concourse.bass_utils import run_bass_kernel_spmd

F32 = mybir.dt.float32
BF16 = mybir.dt.bfloat16
AF = mybir.ActivationFunctionType
ALU = mybir.AluOpType
AX = mybir.AxisListType

ENGS = ("pe", "act", "dve", "pool", "sp")

D = 1024
SEQ = 4096
DFF = 2816
NFF = 22
TT = 512
NCH = 8
INW = 3328
EPS = 1e-6
GN_EPS = 64 * 1e-5


class Sched:
    def __init__(self):
        self.streams = {e: [] for e in ENGS}
        self.count = {e: 0 for e in ENGS}
        self.last_w = {}
        self.readers = {}
        self.waited = {e: {} for e in ENGS}
        self.dma_cnt = {}
        self.sem_names = set(ENGS)

    def _deps(self, eng, reads, writes, skip_same):
        deps = {}

        def add(d):
            s, v = d
            if skip_same and s == eng:
                return
            if deps.get(s, 0) < v:
                deps[s] = v

        for r in reads:
            d = self.last_w.get(r)
            if d is not None:
                add(d)
        for w in writes:
            d = self.last_w.get(w)
            if d is not None:
                add(d)
            for s, v in self.readers.get(w, {}).items():
                add((s, v))
        out = []
        wd = self.waited[eng]
        for s, v in deps.items():
            if wd.get(s, 0) >= v:
                continue
            wd[s] = v
            out.append((s, v))
        return out

    def _record(self, tag, reads, writes):
        s, v = tag
        for r in reads:
            rd = self.readers.setdefault(r, {})
            if rd.get(s, 0) < v:
                rd[s] = v
        for w in writes:
            self.last_w[w] = tag
            self.readers[w] = {}

    def begin_record(self):
        self.rec = []

    def end_record(self):
        r, self.rec = self.rec, None
        return r

    COST = {"pe": 0.15, "act": 0.55, "dve": 0.62, "pool": 0.9, "sp": 2.0}

    def replay_merged(self, a, b, cost_a=None, cost_b=None):
        eng_free = {}
        ready_w = {}
        ready_r = {}
        LAT = 0.12

        def est_start(it, cost):
            eng = it[1]
            if it[0] == "op":
                reads, writes = it[3], it[4]
            else:
                reads, writes = it[4], it[5]
            t = eng_free.get(eng, 0.0)
            for r in reads:
                t = max(t, ready_w.get(r, 0.0) + LAT)
            for w in writes:
                t = max(t, ready_w.get(w, 0.0) + LAT, ready_r.get(w, 0.0) + LAT)
            return t

        def commit(it, cost, t0):
            eng = it[1]
            if it[0] == "op":
                reads, writes = it[3], it[4]
                c = (cost or self.COST).get(eng, 0.5)
            else:
                reads, writes = it[4], it[5]
                c = 2.5
            t1 = t0 + c
            eng_free[eng] = t1 if it[0] == "op" else t0 + 0.1
            for r in reads:
                ready_r[r] = max(ready_r.get(r, 0.0), t1)
                if r.startswith("ps"):
                    ready_w[r] = max(ready_w.get(r, 0.0), t1)
            for w in writes:
                ready_w[w] = t1

        i = j = 0
        na, nb = len(a), len(b)
        while i < na or j < nb:
            if j >= nb:
                pick = 0
            elif i >= na:
                pick = 1
            else:
                ta, tb = est_start(a[i], cost_a), est_start(b[j], cost_b)
                if abs(ta - tb) < 1e-9:
                    pick = 0 if i * nb <= j * na else 1
                else:
                    pick = 0 if ta < tb else 1
            if pick == 0:
                it, cost = a[i], cost_a
                i += 1
            else:
                it, cost = b[j], cost_b
                j += 1
            commit(it, cost, est_start(it, cost))
            if it[0] == "op":
                self.op(*it[1:])
            else:
                self.dma(*it[1:])

    def op(self, eng, fn, reads=(), writes=(), inc=True):
        if getattr(self, "rec", None) is not None:
            self.rec.append(("op", eng, fn, list(reads), list(writes)))
            return
        ex = [r for r in reads if r.startswith("ps")]
        if ex:
            reads = [r for r in reads if not r.startswith("ps")]
            writes = list(writes) + ex
        waits = self._deps(eng, reads, writes, skip_same=(eng == "pe"))
        inc = True
        if inc:
            self.count[eng] += 1
            v = self.count[eng]
        else:
            v = self.count[eng] + 1
        self.streams[eng].append((waits, fn, (eng, 1) if inc else None))
        self._record((eng, v), reads, writes)

    def dma(self, queue, fn, key, reads=(), writes=()):
        if getattr(self, "rec", None) is not None:
            self.rec.append(("dma", queue, fn, key, list(reads), list(writes)))
            return
        waits = self._deps(queue, reads, writes, skip_same=False)
        sname = "dma:" + key
        self.sem_names.add(sname)
        self.dma_cnt[sname] = self.dma_cnt.get(sname, 0) + 16
        v = self.dma_cnt[sname]
        self.streams[queue].append((waits, fn, (sname, 16)))
        self._record((sname, v), reads, writes)

    def final_wait(self, eng, bufs):
        waits = self._deps(eng, bufs, (), skip_same=False)
        self.streams[eng].append((waits, None, None))

    def emit(self, nc, stack):
        sems = {}
        for s in sorted(self.sem_names):
            sems[s] = stack.enter_context(nc.semaphore(s.replace(":", "_")))
        block = stack.enter_context(nc.Block())
        streams = self.streams

        def run(e, stream):
            for waits, fn, inc in stream:
                for s, v in waits:
                    e.wait_ge(sems[s], v)
                if fn is None:
                    continue
                ins = fn(e)
                if inc is not None:
                    ins.then_inc(sems[inc[0]], inc[1])

        @block.tensor
        def _(e):
            run(e, streams["pe"])

        @block.scalar
        def _(e):
            run(e, streams["act"])

        @block.vector
        def _(e):
            run(e, streams["dve"])

        @block.gpsimd
        def _(e):
            run(e, streams["pool"])

        @block.sync
        def _(e):
            run(e, streams["sp"])


CST_IDENT, CST_ONES, CST_UI, CST_LT, CST_LE, CST_BONES, CST_GT = range(7)
NCST = 7


def make_consts():
    p = np.arange(128)[:, None]
    j = np.arange(128)[None, :]
    mats = [
        (p == j), np.ones((128, 128), bool), (p >= j), (p < j), (p <= j),
        (p // 64 == j // 64), (p > j),
    ]
    return np.concatenate([m.astype(np.float32) for m in mats], axis=1)


PP = {}


def _pp_layout():
    off = 0
    for name, n in [("ffn1_pre_g", 8), ("ffn1_post_g", 8), ("mix_pre_g", 8), ("mix_post_g", 8),
                    ("ffn2_pre_g", 8), ("ffn2_post_g", 8), ("shift_mu", 14), ("sb_out_g", 4),
                    ("k_k", 4), ("k_a", 4), ("r_k", 4), ("iclr_a0", 4)]:
        PP[name] = (off, n)
        off += n
    return off


NPP = _pp_layout()


def build_nc(n_tiles=8, stage="full", dbg=False):
    nc = bass.Bass("TRN2", target_bir_lowering=False)
    S = Sched()
    dr = lambda n, shp, dt, kind: nc.dram_tensor(n, shp, dt, kind=kind).ap()
    x_d = dr("x", [SEQ, D], F32, "ExternalInput")
    out_d = dr("out", [SEQ, D], F32, "ExternalOutput")
    cst_d = dr("cst", [128, NCST * 128], F32, "ExternalInput")
    pp_d = dr("pp", [128, NPP], F32, "ExternalInput")
    wts = {}
    for f in ("ffn1", "ffn2"):
        wts[f + "_w_gate"] = dr(f + "_w_gate", [D, DFF], F32, "ExternalInput")
        wts[f + "_w_up"] = dr(f + "_w_up", [D, DFF], F32, "ExternalInput")
        wts[f + "_w_down"] = dr(f + "_w_down", [DFF, D], F32, "ExternalInput")
    w_in_d = dr("w_in", [D, INW], F32, "ExternalInput")
    w_out_d = dr("w_out", [D, D], F32, "ExternalInput")
    decay_w2_d = dr("decay_w2", [64, 512], F32, "ExternalInput")
    iclr_a2_d = dr("iclr_a2", [64, 512], F32, "ExternalInput")
    gate_w2_d = dr("gate_w2", [128, 512], F32, "ExternalInput")
    decay_w0_d = dr("decay_w0", [1, 512], F32, "ExternalInput")
    gn_g_d = dr("gn_g", [1, 512], F32, "ExternalInput")
    gn_b_d = dr("gn_b", [1, 512], F32, "ExternalInput")
    dbg_d = dr("dbg", [n_tiles, 128, 8 * TT], BF16, "ExternalOutput") if dbg else None
    scr = {}
    for f in ("ffn1", "ffn2"):
        scr[f + "_g"] = dr(f + "_gs", [11, 128, 2048], BF16, "Internal")
        scr[f + "_u"] = dr(f + "_us", [11, 128, 2048], BF16, "Internal")
        scr[f + "_d"] = dr(f + "_ds", [16, 128, 1408], BF16, "Internal")
    scr["w_in"] = dr("w_in_s", [13, 128, 2048], BF16, "Internal")
    scr["w_out"] = dr("w_out_s", [4, 128, 2048], BF16, "Internal")

    with ExitStack() as st:
        sb = lambda n, shp, dt: st.enter_context(nc.sbuf_tensor(n, shp, dt))
        NSLOT = 4
        cst = sb("cst_sb", [128, NCST * 128], F32)
        cstb = sb("cstb", [128, NCST * 128], BF16)
        pp = sb("ppar", [128, NPP], F32)
        ghalf = sb("ghalf", [128, 24], F32)
        epsT = sb("epsT", [128, 4], F32)
        onem = sb("onem", [128, 14], F32)
        xT = sb("xT_sb", [128, NCH, TT], F32)
        xnT = sb("xnT", [128, NCH, TT], BF16)
        hT = sb("hT", [128, NFF, TT], BF16)
        fT = sb("fT", [128, NCH, TT], F32)
        wslot = [sb("wslot%d" % i, [128, 2048], BF16) for i in range(NSLOT)]
        xio = [sb("xio%d" % i, [128, D], F32) for i in range(2)]
        sqb = [sb("sqb%d" % i, [128, TT], BF16) for i in range(2)]
        silb = [sb("silb%d" % i, [128, TT], BF16) for i in range(2)]
        rstd = sb("rstd", [128, TT], F32)
        tmpf = [sb("tmpf%d" % i, [128, TT], F32) for i in range(2)]
        kTh = sb("kTh", [128, 4, SEQ], BF16)
        dvh = sb("dvh", [128, SEQ // 128, 512], BF16)
        gbuf0 = sb("gbuf0", [128, 4, TT], BF16)
        gbuf1 = sb("gbuf1", [128, 4, TT], BF16)
        mixT = sb("mixT", [128, NCH, TT], BF16)
        halfA = [sb("halfA%d" % i, [128, 4, 128], BF16) for i in range(2)]
        halfB = [sb("halfB%d" % i, [128, 4, 128], BF16) for i in range(2)]
        halfK = [sb("halfK%d" % i, [128, 4, 128], BF16) for i in range(2)]
        Tfin = sb("Tfin", [128, 8, 128], BF16)
        Mbr = sb("Mbr", [128, 8, 128], BF16)
        Mkr = sb("Mkr", [128, 8, 128], BF16)
        pst = sb("pst", [128, TT + 1], F32)
        lora_w = sb("lora_w", [128, 512], BF16)
        gate_wb = sb("gate_wb", [128, 512], BF16)
        w0b = sb("w0b", [128, 512], F32)
        gng = sb("gng", [128, 512], F32)
        gnb = sb("gnb", [128, 512], F32)
        Sf = sb("Sf", [128, 4, 64], F32)
        Sb = sb("Sb", [128, 4, 64], BF16)
        bonus = sb("bonus", [128, 4, 8], F32)
        xn_carry = sb("xn_carry", [128, 8, 1], BF16)
        prw_carry = sb("prw_carry", [128, 14], F32)
        nLgC = sb("nLgC", [128, 4], F32)
        gamC = sb("gamC", [128, 4], F32)
        gst = sb("gst", [128, 4, 8], F32)
        ps = [st.enter_context(nc.psum_tensor("ps%d" % i, [128, 512], F32)) for i in range(8)]
        psb = [p.bitcast(BF16) for p in ps]

        def cI(k):
            return cst[:, k * 128:(k + 1) * 128]

        def cB(k):
            return cstb[:, k * 128:(k + 1) * 128]

        def ppc(name, c):
            o, n = PP[name]
            return pp[:, o + c:o + c + 1]

        def hTc(i):
            return hT[:, i, :]

        def fTc(i):
            return fT[:, i, :]

        S.dma("sp", lambda e: e.dma_start(out=cst[:], in_=cst_d), "cst", writes=["cst"])
        S.dma("sp", lambda e: e.dma_start(out=pp[:], in_=pp_d), "pp", writes=["pp"])
        S.op("dve", lambda e: e.tensor_copy(out=cstb[:], in_=cst[:]), reads=["cst"], writes=["cstb"])
        S.op("dve", lambda e: e.memset(epsT[:, 0:1], EPS), writes=["epsT"])
        S.op("dve", lambda e: e.memset(epsT[:, 1:2], GN_EPS), writes=["epsT"])
        S.op("dve", lambda e: e.memset(epsT[:, 2:3], -0.5), writes=["epsT"])
        S.op("dve", lambda e: e.memset(epsT[:, 3:4], 1.0), writes=["epsT"])
        for i, nm in enumerate(("ffn1_post_g", "mix_post_g", "ffn2_post_g")):
            o, n = PP[nm]
            sc = 1.0 if nm == "mix_post_g" else 0.5
            S.op("dve", (lambda i, o, sc: lambda e: e.tensor_scalar(
                out=ghalf[:, i * 8:(i + 1) * 8], in0=pp[:, o:o + 8], scalar1=sc, scalar2=None, op0=ALU.mult))(i, o, sc),
                reads=["pp"], writes=["ghalf"])
        o_mu = PP["shift_mu"][0]
        S.op("dve", lambda e: e.tensor_scalar(out=onem[:], in0=pp[:, o_mu:o_mu + 14], scalar1=-1.0, scalar2=1.0,
                                              op0=ALU.mult, op1=ALU.add), reads=["pp"], writes=["onem"])
        S.dma("sp", lambda e: e.dma_start(out=xio[1][0:64, 0:512], in_=decay_w2_d), "sw0", writes=["xio1"])
        S.dma("sp", lambda e: e.dma_start(out=xio[1][64:128, 0:512], in_=iclr_a2_d), "sw1", writes=["xio1"])
        S.dma("sp", lambda e: e.dma_start(out=xio[1][:, 512:1024], in_=gate_w2_d), "sw2", writes=["xio1"])
        S.op("dve", lambda e: e.tensor_copy(out=lora_w[:], in_=xio[1][:, 0:512]), reads=["xio1"], writes=["lora_w"])
        S.op("dve", lambda e: e.tensor_copy(out=gate_wb[:], in_=xio[1][:, 512:1024]), reads=["xio1"], writes=["gate_wb"])
        for i, (t, d_, nm) in enumerate(((w0b, decay_w0_d, "w0b"), (gng, gn_g_d, "gng"), (gnb, gn_b_d, "gnb"))):
            src = bass.AP(d_.tensor, 0, [[0, 128], [1, 512]])
            S.dma("sp", (lambda t, src: lambda e: e.dma_start(out=t[:], in_=src))(t, src), "bc%d" % i, writes=[nm])
        S.op("pool", lambda e: e.memset(Sf[:], 0.0), writes=["Sf"])
        S.op("pool", lambda e: e.memset(Sb[:], 0.0), writes=["Sb"])
        S.op("pool", lambda e: e.memset(xn_carry[:], 0.0), writes=["xn_carry"])
        S.op("pool", lambda e: e.memset(prw_carry[:], 0.0), writes=["prw_carry"])

        cast_i = [0]

        def cast_dma(dst_v, src_v, nm):
            k = cast_i[0] % 8
            cast_i[0] += 1
            S.dma("pool", lambda e: e.dma_start(out=dst_v, in_=src_v), "cast%d" % k,
                  writes=["scr_" + nm, "casttok%d" % k])

        def cast_ffn(f):
            for g in range(11):
                for which, key in (("_g", "_w_gate"), ("_u", "_w_up")):
                    src = wts[f + key]
                    src_v = src.rearrange("(c p) f -> p c f", p=128)[:, :, g * 256:(g + 1) * 256]
                    dst_v = scr[f + which][g].rearrange("p (c j) -> p c j", c=8)
                    cast_dma(dst_v, src_v, "%s%s%d" % (f, which, g))
            src = wts[f + "_w_down"]
            for dc in range(8):
                for hf in range(2):
                    src_v = src.rearrange("(c p) d -> p c d", p=128)[:, hf * 11:(hf + 1) * 11, dc * 128:(dc + 1) * 128]
                    dst_v = scr[f + "_d"][dc * 2 + hf].rearrange("p (c j) -> p c j", c=11)
                    cast_dma(dst_v, src_v, "%s_d%d" % (f, dc * 2 + hf))

        def cast_mix():
            for g in range(13):
                src_v = w_in_d.rearrange("(c p) f -> p c f", p=128)[:, :, g * 256:(g + 1) * 256]
                dst_v = scr["w_in"][g].rearrange("p (c j) -> p c j", c=8)
                cast_dma(dst_v, src_v, "w_in%d" % g)
            for g in range(4):
                src_v = w_out_d.rearrange("(c p) f -> p c f", p=128)[:, :, g * 256:(g + 1) * 256]
                dst_v = scr["w_out"][g].rearrange("p (c j) -> p c j", c=8)
                cast_dma(dst_v, src_v, "w_out%d" % g)

        cast_ffn("ffn1")
        if stage in ("mix", "full"):
            cast_mix()
        cast_ffn("ffn2")

        slot_i = [0]

        def wload(scr_ap, nm, ncols):
            k = slot_i[0] % NSLOT
            slot_i[0] += 1
            S.dma("sp", (lambda k, a, n: lambda e: e.dma_start(out=wslot[k][:, 0:n], in_=a))(k, scr_ap, ncols),
                  "ws%d" % k, reads=["scr_" + nm], writes=["wslot%d" % k])
            return k

        XC = lambda c: ["x%d_%d" % (c, b_) for b_ in range(4)]
        XB = lambda b_: ["x%d_%d" % (c_, b_) for c_ in range(NCH)]
        def finish_rstd(stat_bank, inv_n, eps_col=0, rows=None):
            S.op("act", lambda e: e.activation(out=rstd[:], in_=ps[stat_bank][:, :], func=AF.Ln,
                                               bias=epsT[:, eps_col:eps_col + 1], scale=inv_n),
                 reads=["ps%d" % stat_bank, "epsT"], writes=["rstd"])
            S.op("act", lambda e: e.activation(out=rstd[:], in_=rstd[:], func=AF.Exp, scale=-0.5),
                 reads=["rstd"], writes=["rstd"])

        def pre_norm(gname):
            for c in range(NCH):
                q = sqb[c % 2]
                qn = "sqb%d" % (c % 2)
                if c % 2 == 0:
                    S.op("dve", (lambda c, q: lambda e: e.tensor_tensor(out=q[:], in0=xT[:, c, :], in1=xT[:, c, :],
                                                                        op=ALU.mult))(c, q), reads=XC(c), writes=[qn])
                else:
                    S.op("act", (lambda c, q: lambda e: e.activation(out=q[:], in_=xT[:, c, :], func=AF.Square))(c, q),
                         reads=XC(c), writes=[qn])
                S.op("pe", (lambda c, q: lambda e: e.matmul(ps[6][:, :], lhsT=cB(CST_ONES), rhs=q[:],
                                                            start=(c == 0), stop=(c == NCH - 1)))(c, q),
                     reads=[qn, "cstb"], writes=["ps6"])
            finish_rstd(6, 1.0 / D)
            for c in range(NCH):
                S.op("dve", (lambda c: lambda e: e.scalar_tensor_tensor(
                    out=xnT[:, c, :], in0=xT[:, c, :], scalar=ppc(gname, c), in1=rstd[:],
                    op0=ALU.mult, op1=ALU.mult))(c), reads=XC(c) + ["rstd", "pp"], writes=["xnT%d" % c])

        def post_norm_residual(gi):
            finish_rstd(6, 1.0 / D)
            for c in range(NCH):
                S.op("dve", (lambda c: lambda e: e.scalar_tensor_tensor(
                    out=fT[:, c, :], in0=fT[:, c, :], scalar=ghalf[:, gi * 8 + c:gi * 8 + c + 1], in1=rstd[:],
                    op0=ALU.mult, op1=ALU.mult))(c), reads=["fT%d" % c, "rstd", "ghalf"], writes=["fT%d" % c])
                S.op("dve", (lambda c: lambda e: e.tensor_tensor(out=xT[:, c, :], in0=xT[:, c, :], in1=fT[:, c, :],
                                                                 op=ALU.add))(c),
                     reads=["fT%d" % c] + XC(c), writes=XC(c))

        def out_chunk_epilogue(dc, bd):
            S.op("dve", (lambda dc, bd: lambda e: e.tensor_copy(out=fT[:, dc, :], in_=ps[bd][:, :]))(dc, bd),
                 reads=["ps%d" % bd], writes=["fT%d" % dc])
            q = sqb[dc % 2]
            qn = "sqb%d" % (dc % 2)
            S.op("act", (lambda q, bd: lambda e: e.activation(out=q[:], in_=ps[bd][:, :], func=AF.Square))(q, bd),
                 reads=["ps%d" % bd], writes=[qn])

        def out_chunk_stats(dc):
            q = sqb[dc % 2]
            qn = "sqb%d" % (dc % 2)
            S.op("pe", (lambda dc, q: lambda e: e.matmul(ps[6][:, :], lhsT=cB(CST_ONES), rhs=q[:],
                                                         start=(dc == 0), stop=(dc == NCH - 1)))(dc, q),
                 reads=[qn, "cstb"], writes=["ps6"])

        def ffn(f, gpre, gi_post):
            pre_norm(gpre)
            xn_reads = ["xnT%d" % c for c in range(NCH)]
            for g in range(11):
                kg = wload(scr[f + "_g"][g], "%s_g%d" % (f, g), 2048)
                ku = wload(scr[f + "_u"][g], "%s_u%d" % (f, g), 2048)
                wg = wslot[kg][:, 0:2048].rearrange("p (c j) -> p c j", c=8)
                wu = wslot[ku][:, 0:2048].rearrange("p (c j) -> p c j", c=8)
                for j in range(2):
                    fc = 2 * g + j
                    bg, bu = fc % 2, 2 + fc % 2
                    for c in range(NCH):
                        S.op("pe", (lambda c, j, wg, bg: lambda e: e.matmul(
                            ps[bg][:, :], lhsT=wg[:, c, j * 128:(j + 1) * 128], rhs=xnT[:, c, :],
                            start=(c == 0), stop=(c == NCH - 1)))(c, j, wg, bg),
                            reads=["wslot%d" % kg] + xn_reads, writes=["ps%d" % bg])
                    for c in range(NCH):
                        S.op("pe", (lambda c, j, wu, bu: lambda e: e.matmul(
                            ps[bu][:, :], lhsT=wu[:, c, j * 128:(j + 1) * 128], rhs=xnT[:, c, :],
                            start=(c == 0), stop=(c == NCH - 1)))(c, j, wu, bu),
                            reads=["wslot%d" % ku] + xn_reads, writes=["ps%d" % bu])
                    sl = silb[fc % 2]
                    sn = "silb%d" % (fc % 2)
                    S.op("act", (lambda sl, bg: lambda e: e.activation(out=sl[:], in_=ps[bg][:, :], func=AF.Silu))(sl, bg),
                         reads=["ps%d" % bg], writes=[sn])
                    S.op("dve", (lambda sl, bu, fc: lambda e: e.tensor_tensor(
                        out=hT[:, fc, :], in0=sl[:], in1=ps[bu][:, :], op=ALU.mult))(sl, bu, fc),
                        reads=[sn, "ps%d" % bu], writes=["hT%d" % fc])
            h_reads = ["hT%d" % i for i in range(NFF)]
            for dc in range(NCH):
                bd = 4 + dc % 2
                for hf in range(2):
                    kd = wload(scr[f + "_d"][dc * 2 + hf], "%s_d%d" % (f, dc * 2 + hf), 1408)
                    wd = wslot[kd][:, 0:1408].rearrange("p (c j) -> p c j", c=11)
                    for i in range(11):
                        fc = hf * 11 + i
                        S.op("pe", (lambda fc, i, wd, bd: lambda e: e.matmul(
                            ps[bd][:, :], lhsT=wd[:, i, :], rhs=hT[:, fc, :],
                            start=(fc == 0), stop=(fc == NFF - 1)))(fc, i, wd, bd),
                            reads=["wslot%d" % kd] + h_reads, writes=["ps%d" % bd])
                out_chunk_epilogue(dc, bd)
                if dc >= 1:
                    out_chunk_stats(dc - 1)
            out_chunk_stats(NCH - 1)
            post_norm_residual(gi_post)

        def io_tile(Ts, Tl):
            for tb in range(4):
                if Tl < n_tiles:
                    r0 = Tl * TT + tb * 128
                    S.dma("sp", (lambda r0: lambda e: e.dma_start(out=xio[0][:], in_=x_d[r0:r0 + 128, :]))(r0),
                          "xio0", writes=["xio0"])
                if Ts >= 0:
                    r0 = Ts * TT + tb * 128
                    xo = xio[1]
                    for half in range(2):
                        for cc in range(4):
                            c = half * 4 + cc
                            S.op("pe", (lambda c, cc, tb: lambda e: e.transpose(
                                out=ps[5][:, cc * 128:(cc + 1) * 128], in_=xT[:, c, tb * 128:(tb + 1) * 128],
                                identity=cI(CST_IDENT)))(c, cc, tb),
                                reads=XB(tb) + ["cst"], writes=["ps5"])
                        S.op("act", (lambda half: lambda e: e.copy(out=xo[:, half * 512:(half + 1) * 512],
                                                                   in_=ps[5][:, :]))(half),
                             reads=["ps5"], writes=["xio1"])
                    S.dma("sp", (lambda r0: lambda e: e.dma_start(out=out_d[r0:r0 + 128, :], in_=xo[:]))(r0),
                          "outst", reads=["xio1"], writes=["out_hbm"])
                if Tl < n_tiles:
                    xi = xio[0]
                    for half in range(2):
                        for cc in range(4):
                            c = half * 4 + cc
                            S.op("pe", (lambda xi, c, cc: lambda e: e.transpose(
                                out=ps[7][:, cc * 128:(cc + 1) * 128], in_=xi[:, c * 128:(c + 1) * 128],
                                identity=cI(CST_IDENT)))(xi, c, cc),
                                reads=["xio0", "cst"], writes=["ps7"])
                        S.op("dve", (lambda half, tb: lambda e: e.tensor_copy(
                            out=xT[:, half * 4:half * 4 + 4, tb * 128:(tb + 1) * 128],
                            in_=ps[7][:, :].rearrange("p (c t) -> p c t", c=4)))(half, tb),
                            reads=["ps7"], writes=["x%d_%d" % (c_, tb) for c_ in range(half * 4, half * 4 + 4)])

        qT, vT = gbuf0, gbuf1
        vtok = xnT[:, 0:4, :]
        GB0 = ["xnT%d" % i for i in range(4)]
        xn_reads = ["xnT%d" % c for c in range(NCH)]

        def proj_fm(wt, kslot, j, bank):
            for c in range(NCH):
                S.op("pe", (lambda c: lambda e: e.matmul(
                    ps[bank][:, :], lhsT=wt[:, c, j * 128:(j + 1) * 128], rhs=xnT[:, c, :],
                    start=(c == 0), stop=(c == NCH - 1)))(c),
                    reads=["wslot%d" % kslot] + xn_reads, writes=["ps%d" % bank])

        def mixer_proj_sb(T):
            dx = mixT[:, :, :]
            dxn = ["mixT%d" % i for i in range(8)]
            S.op("dve", lambda e: e.tensor_tensor(out=dx[:, :, 1:TT], in0=xnT[:, :, 0:TT - 1], in1=xnT[:, :, 1:TT],
                                                  op=ALU.subtract), reads=xn_reads, writes=dxn)
            S.op("dve", lambda e: e.tensor_tensor(out=dx[:, :, 0:1], in0=xn_carry[:, :, 0:1], in1=xnT[:, :, 0:1],
                                                  op=ALU.subtract), reads=xn_reads + ["xn_carry"], writes=dxn)
            S.op("dve", lambda e: e.tensor_copy(out=xn_carry[:, :, 0:1], in_=xnT[:, :, TT - 1:TT]),
                 reads=xn_reads, writes=["xn_carry"])
            bank = [0]
            for gi in range(6):
                k = wload(scr["w_in"][gi], "w_in%d" % gi, 2048)
                wt = wslot[k][:, 0:2048].rearrange("p (c j) -> p c j", c=8)
                for j in range(2):
                    cc = (gi % 2) * 2 + j
                    b = bank[0] % 2
                    bank[0] += 1
                    proj_fm(wt, k, j, b)
                    if gi < 2:
                        S.op("act", (lambda cc, b: lambda e: e.mul(out=qT[:, cc, :], in_=ps[b][:, :], mul=0.125))(cc, b),
                             reads=["ps%d" % b], writes=["qT%d" % cc])
                    elif gi < 4:
                        S.op("act", (lambda cc, b: lambda e: e.copy(out=kTh[:, cc, T * TT:(T + 1) * TT],
                                                                    in_=ps[b][:, :]))(cc, b),
                             reads=["ps%d" % b], writes=["kT%d_%d" % (cc, T)])
                    else:
                        S.op("act", (lambda cc, b: lambda e: e.copy(out=vT[:, cc, :], in_=ps[b][:, :]))(cc, b),
                             reads=["ps%d" % b], writes=["vT%d" % cc])
                if gi >= 4:
                    for tb in range(4):
                        b = 2 + tb % 2
                        for c in range(NCH):
                            S.op("pe", (lambda c, tb, b, wt: lambda e: e.matmul(
                                ps[b][:, 0:256], lhsT=dx[:, c, tb * 128:(tb + 1) * 128], rhs=wt[:, c, :],
                                start=(c == 0), stop=(c == NCH - 1)))(c, tb, b, wt),
                                reads=["wslot%d" % k] + dxn, writes=["ps%d" % b])
                        S.op("dve", (lambda tb, b, gi: lambda e: e.tensor_copy(
                            out=dvh[:, 4 * T + tb, (gi - 4) * 256:(gi - 3) * 256], in_=ps[b][:, 0:256]))(tb, b, gi),
                            reads=["ps%d" % b], writes=["dv%d" % (4 * T + tb)])

        def sb_attention(T):
            nkb = 4 * T + 4
            w1, w2, w3 = wslot[1], wslot[2], wslot[3]
            spb = [w1[:, 0:512], w1[:, 512:1024]]
            Csum = w1[:, 1024:1536]
            Pb = [w1[:, 1536:2048], w2[:, 0:512]]
            sqA = w2[:, 512:1024]
            efb = [w2[:, 1024:2048].bitcast(F32), w3[:, 0:1024].bitcast(F32)]
            oT = w3[:, 1024:2048].bitcast(F32)
            WS_ = {"a_sp0": "wslot1", "a_sp1": "wslot1", "a_cs": "wslot1", "a_P0": "wslot1", "a_P1": "wslot2",
                   "a_sq": "wslot2", "a_ef0": "wslot2", "a_ef1": "wslot3", "a_oT": "wslot3"}

            def RD(names):
                return list(names) + sorted({WS_[n] for n in names if n in WS_})

            S.op("pool", lambda e: e.memset(Csum, 0.0), writes=["wslot1", "a_cs"])
            S.op("pool", lambda e: e.memset(sqA, 0.0), writes=["wslot2", "a_sq"])
            S.op("pool", lambda e: e.memset(oT, 0.0), writes=["wslot3", "a_oT"])
            items = []
            for c in range(4):
                for half in range(2):
                    for kb in reversed(range(nkb)):
                        items.append((c, half, kb))

            def geom(kb):
                diag = kb >= 4 * T
                n0 = (kb - 4 * T) * 128 if diag else 0
                return diag, n0, slice(n0, TT)

            def stage1(i):
                c, half, kb = items[i]
                hs = slice(64 * half, 64 * half + 64)
                diag, n0, cols = geom(kb)
                zb = 0
                ef, sp = efb[i % 2], spb[i % 2]
                efn, spn = "a_ef%d" % (i % 2), "a_sp%d" % (i % 2)
                S.op("pe", lambda e: e.matmul(
                    ps[zb][:, cols], lhsT=kTh[hs, c, kb * 128:(kb + 1) * 128], rhs=qT[hs, c, cols],
                    start=True, stop=True), reads=["kT%d_%d" % (c, kb // 4), "qT%d" % c], writes=["ps%d" % zb])
                S.op("act", lambda e: e.activation(out=ef[:, cols], in_=ps[zb][:, cols], func=AF.Exp),
                     reads=RD(["ps%d" % zb, efn]), writes=[efn])
                S.op("act", lambda e: e.activation(out=sp[:, cols], in_=ef[:, cols], func=AF.Ln,
                                                   bias=epsT[:, 3:4], scale=1.0),
                     reads=RD([efn, "epsT", spn]), writes=[spn])
                if diag:
                    S.op("dve", lambda e: e.tensor_tensor(out=sp[:, n0:n0 + 128], in0=sp[:, n0:n0 + 128],
                                                          in1=cB(CST_LT), op=ALU.mult),
                         reads=RD([spn, "cstb"]), writes=[spn])

            def stage2(i):
                c, half, kb = items[i]
                hs = slice(64 * half, 64 * half + 64)
                diag, n0, cols = geom(kb)
                first = kb == nkb - 1
                cb_ = 1
                ob = 2 + half
                sp, P = spb[i % 2], Pb[i % 2]
                spn, Pn = "a_sp%d" % (i % 2), "a_P%d" % (i % 2)
                if first:
                    S.op("pool", lambda e: e.memset(Csum, 0.0), reads=RD(["a_cs"]), writes=["a_cs"])
                S.op("pe", lambda e: e.matmul(ps[cb_][:, cols], lhsT=cB(CST_UI), rhs=sp[:, cols],
                                              start=True, stop=False), reads=RD([spn, "cstb"]), writes=["ps%d" % cb_])
                S.op("pe", lambda e: e.matmul(ps[cb_][:, cols], lhsT=cB(CST_ONES), rhs=Csum[:, cols],
                                              start=False, stop=True), reads=RD(["a_cs", "cstb"]), writes=["ps%d" % cb_])
                S.op("dve", lambda e: e.tensor_tensor(out=Csum[:, cols], in0=Csum[:, cols], in1=sp[:, cols],
                                                      op=ALU.add), reads=RD([spn, "a_cs"]), writes=["a_cs"])
                if first and n0 > 0:
                    S.op("pool", lambda e: e.memset(P[:, 0:n0], 0.0), reads=RD([Pn]), writes=[Pn])
                S.op("act", lambda e: e.activation(out=P[:, cols], in_=ps[cb_][:, cols], func=AF.Exp, scale=-1.0),
                     reads=RD(["ps%d" % cb_, Pn]), writes=[Pn])
                if diag:
                    S.op("dve", lambda e: e.tensor_tensor(out=P[:, n0:n0 + 128], in0=P[:, n0:n0 + 128],
                                                          in1=cB(CST_LE), op=ALU.mult),
                         reads=RD([Pn, "cstb"]), writes=[Pn])

            def stage3(i):
                c, half, kb = items[i]
                hs = slice(64 * half, 64 * half + 64)
                diag, n0, cols = geom(kb)
                first = kb == nkb - 1
                ob = 2 + half
                P = Pb[i % 2]
                Pn = "a_P%d" % (i % 2)
                pcols = slice(0, TT) if first else cols
                S.op("pe", lambda e: e.matmul(
                    ps[ob][:, pcols], lhsT=dvh[:, kb, c * 128:(c + 1) * 128], rhs=P[:, pcols],
                    start=first, stop=(kb == 0), skip_group_check=True),
                    reads=RD([Pn, "dv%d" % kb]), writes=["ps%d" % ob])
                if kb == 0:
                    S.op("dve", lambda e: e.tensor_tensor(out=oT[hs, :], in0=ps[ob][hs, :], in1=vT[hs, c, :],
                                                          op=ALU.add),
                         reads=RD(["ps%d" % ob, "vT%d" % c, "a_oT"]), writes=["a_oT"])
                    if half == 1:
                        head_norm(c)

            def head_norm(c):
                S.op("dve", lambda e: e.tensor_tensor(out=sqA, in0=oT, in1=oT, op=ALU.mult),
                     reads=RD(["a_oT", "a_sq"]), writes=["a_sq"])
                S.op("pe", lambda e: e.matmul(ps[1][:, :], lhsT=cB(CST_BONES), rhs=sqA, start=True, stop=True),
                     reads=RD(["a_sq", "cstb"]), writes=["ps1"])
                finish_rstd(1, 1.0 / 64)
                S.op("dve", lambda e: e.scalar_tensor_tensor(
                    out=mixT[:, c, :], in0=oT, scalar=ppc("sb_out_g", c), in1=rstd[:],
                    op0=ALU.mult, op1=ALU.mult), reads=RD(["a_oT", "rstd", "pp"]), writes=["mixT%d" % c])

            n = len(items)
            for s_ in range(n + 2):
                if s_ < n:
                    stage1(s_)
                if 1 <= s_ <= n:
                    stage2(s_ - 1)
                if s_ >= 2:
                    stage3(s_ - 2)

        def rwkv_prep(T):
            order = [12, 6, 7, 8, 9, 10, 11]
            bank = [0]
            r_b = lambda c: hTc(c)
            k2_b = lambda c: hTc(4 + c)
            kkn_b = lambda c: hTc(8 + c)
            b_b = lambda c: hTc(12 + c)
            vr_b = lambda c: hTc(16 + c)
            twa, sgT = hTc(20), hTc(21)
            chunks = [(gi, j) for gi in order for j in range(2)]
            slots = {}

            def emit_proj(i):
                gi, j = chunks[i]
                if j == 0:
                    k = wload(scr["w_in"][gi], "w_in%d" % gi, 2048)
                    slots[gi] = (k, wslot[k][:, 0:2048].rearrange("p (c j) -> p c j", c=8))
                k, wt = slots[gi]
                proj_fm(wt, k, j, 4 + i % 2)

            def post_chunk(ch, b):
                S.op("act", (lambda b: lambda e: e.copy(out=pst[:, 1:TT + 1], in_=ps[b][:, :]))(b),
                     reads=["ps%d" % b], writes=["pst"])
                S.op("pool", (lambda ch: lambda e: e.tensor_copy(out=pst[:, 0:1], in_=prw_carry[:, ch:ch + 1]))(ch),
                     reads=["prw_carry"], writes=["pst"])
                pm = fTc(0)
                S.op("dve", (lambda ch: lambda e: e.tensor_scalar(
                    out=fTc(1), in0=pst[:, 0:TT], scalar1=ppc("shift_mu", ch), scalar2=None, op0=ALU.mult))(ch),
                    reads=["pst", "pp"], writes=["fT1"])
                if ch < 4:
                    dst, dn = r_b(ch), "hT%d" % ch
                elif ch < 8:
                    dst, dn = fTc(2), "fT2"
                elif ch < 12:
                    dst, dn = vr_b(ch - 8), "hT%d" % (16 + ch - 8)
                else:
                    dst, dn = pm, "fT0"
                S.op("dve", (lambda ch, dst: lambda e: e.scalar_tensor_tensor(
                    out=dst, in0=pst[:, 1:TT + 1], scalar=onem[:, ch:ch + 1], in1=fTc(1),
                    op0=ALU.mult, op1=ALU.add))(ch, dst), reads=["pst", "onem", "fT1"], writes=[dn])
                S.op("pool", (lambda ch: lambda e: e.tensor_copy(out=prw_carry[:, ch:ch + 1], in_=pst[:, TT:TT + 1]))(ch),
                     reads=["pst"], writes=["prw_carry"])
                if ch == 12:
                    S.op("act", lambda e: e.activation(out=twa[0:64, :], in_=pm[0:64, :], func=AF.Tanh),
                         reads=["fT0"], writes=["hT20"])
                    S.op("act", lambda e: e.copy(out=twa[64:128, :], in_=pm[64:128, :]),
                         reads=["fT0"], writes=["hT20"])
                elif ch == 13:
                    S.op("act", lambda e: e.activation(out=sgT, in_=pm, func=AF.Sigmoid),
                         reads=["fT0"], writes=["hT21"])
                elif 4 <= ch < 8:
                    c = ch - 4
                    kr = fTc(2)
                    a_f, kk, t3, t4 = fTc(3), fTc(4), fTc(5), fTc(6)
                    S.op("pe", (lambda c: lambda e: e.matmul(
                        ps[6][:, :], lhsT=lora_w[64:128, c * 128:(c + 1) * 128], rhs=twa[64:128, :],
                        start=True, stop=True))(c), reads=["lora_w", "hT20"], writes=["ps6"])
                    S.op("act", (lambda c: lambda e: e.activation(out=a_f, in_=ps[6][:, :], func=AF.Sigmoid,
                                                                  bias=ppc("iclr_a0", c), scale=1.0))(c),
                         reads=["ps6", "pp"], writes=["fT3"])
                    S.op("dve", (lambda c: lambda e: e.tensor_scalar(out=kk, in0=kr, scalar1=ppc("k_k", c),
                                                                     scalar2=None, op0=ALU.mult))(c),
                         reads=["fT2", "pp"], writes=["fT4"])
                    S.op("dve", lambda e: e.tensor_tensor(out=t3, in0=kk, in1=kk, op=ALU.mult),
                         reads=["fT4"], writes=["fT5"])
                    S.op("pe", lambda e: e.matmul(ps[7][:, :], lhsT=cI(CST_BONES), rhs=t3, start=True, stop=True),
                         reads=["fT5", "cst"], writes=["ps7"])
                    S.op("dve", lambda e: e.tensor_scalar(out=t3, in0=ps[7][:, :], scalar1=1e-24, scalar2=None,
                                                          op0=ALU.max), reads=["ps7"], writes=["fT5"])
                    S.op("act", lambda e: e.activation(out=t3, in_=t3, func=AF.Ln), reads=["fT5"], writes=["fT5"])
                    S.op("act", lambda e: e.activation(out=t3, in_=t3, func=AF.Exp, scale=-0.5),
                         reads=["fT5"], writes=["fT5"])
                    S.op("dve", (lambda c: lambda e: e.tensor_tensor(out=kkn_b(c), in0=kk, in1=t3, op=ALU.mult))(c),
                         reads=["fT4", "fT5"], writes=["hT%d" % (8 + c)])
                    S.op("dve", (lambda c: lambda e: e.tensor_scalar(out=t4, in0=a_f, scalar1=-1.0,
                                                                     scalar2=ppc("k_a", c), op0=ALU.add,
                                                                     op1=ALU.mult))(c),
                         reads=["fT3", "pp"], writes=["fT6"])
                    S.op("dve", (lambda c: lambda e: e.scalar_tensor_tensor(
                        out=k2_b(c), in0=t4, scalar=1.0, in1=kr, op0=ALU.add, op1=ALU.mult))(c),
                        reads=["fT6", "fT2"], writes=["hT%d" % (4 + c)])
                    S.op("dve", (lambda c: lambda e: e.tensor_tensor(out=b_b(c), in0=kkn_b(c), in1=a_f,
                                                                     op=ALU.mult))(c),
                         reads=["hT%d" % (8 + c), "fT3"], writes=["hT%d" % (12 + c)])
                    S.op("dve", (lambda c: lambda e: e.scalar_tensor_tensor(
                        out=silb[0][:], in0=r_b(c), scalar=ppc("r_k", c), in1=k2_b(c),
                        op0=ALU.mult, op1=ALU.mult))(c),
                        reads=["hT%d" % c, "hT%d" % (4 + c), "pp"], writes=["silb0"])
                    hsel = cstb[:, CST_BONES * 128:(CST_BONES + 1) * 128:64]
                    for tb in range(4):
                        S.op("pe", (lambda tb: lambda e: e.matmul(
                            ps[7][:, tb * 2:tb * 2 + 2], lhsT=silb[0][:, tb * 128:(tb + 1) * 128], rhs=hsel,
                            start=True, stop=True))(tb), reads=["silb0", "cstb"], writes=["ps7"])
                    S.op("dve", (lambda c: lambda e: e.tensor_copy(
                        out=bonus[:, :, 2 * c:2 * c + 2],
                        in_=ps[7][:, 0:8].rearrange("p (t j) -> p t j", t=4)))(c),
                        reads=["ps7"], writes=["bonus"])

            emit_proj(0)
            for i in range(len(chunks)):
                if i + 1 < len(chunks):
                    emit_proj(i + 1)
                gi, j = chunks[i]
                post_chunk((gi - 6) * 2 + j, 4 + i % 2)

        def rwkv_prep_tail():
            vr_b = lambda c: hTc(16 + c)
            for c in range(4):
                for tb in range(4):
                    S.op("pe", (lambda c, tb: lambda e: e.transpose(
                        out=psb[7][:, tb * 128:(tb + 1) * 128], in_=vr_b(c)[:, tb * 128:(tb + 1) * 128],
                        identity=cB(CST_IDENT)))(c, tb),
                        reads=["hT%d" % (16 + c), "cstb"], writes=["ps7"])
                S.op("dve", (lambda c: lambda e: e.tensor_copy(
                    out=vtok[:, :, c * 128:(c + 1) * 128],
                    in_=psb[7][:, 0:512].rearrange("p (t j) -> p t j", t=4)))(c),
                    reads=["ps7"], writes=GB0)


        def rwkv_chunk(T, tb):
            tsl = slice(tb * 128, (tb + 1) * 128)
            r4 = hT[:, 0:4, tsl]
            k4 = hT[:, 4:8, tsl]
            kkn4 = hT[:, 8:12, tsl]
            b4 = hT[:, 12:16, tsl]
            rn_, kn_, kknn_, bn_ = (["hT%d" % (o + i) for i in range(4)] for o in (0, 4, 8, 12))
            twa, sgT = hTc(20), hTc(21)
            V4 = lambda t: t.rearrange("p (c j) -> p c j", c=4)
            kt, bt = V4(hT[:, 17, :]), V4(hT[:, 18, :])
            rt_bd = fT[:, 7, :].bitcast(BF16).rearrange("p (c j) -> p c j", c=4)
            at_bd = fT[:, 6, :].bitcast(BF16).rearrange("p (c j) -> p c j", c=4)
            rtn, ktn, btn, atn = "fT7", "hT17", "hT18", "fT6"
            HA, HB = slice(0, 64), slice(64, 128)
            Kh, Bh = V4(xnT[:, 4, :]), V4(xnT[:, 5, :])
            Kht, Bht = xnT[:, 6, :], xnT[:, 7, :]
            lw, tsp = fTc(0), fTc(1)
            Et, y = fTc(2), fTc(3)
            S.op("pe", lambda e: e.matmul(ps[4][:, :], lhsT=twa[0:64, tsl], rhs=lora_w[0:64, :], start=True, stop=True),
                 reads=["hT20", "lora_w"], writes=["ps4"])
            S.op("dve", lambda e: e.tensor_tensor(out=lw, in0=ps[4][:, :], in1=w0b[:], op=ALU.add),
                 reads=["ps4", "w0b"], writes=["fT0"])
            S.op("act", lambda e: e.activation(out=tsp, in_=lw, func=AF.Exp, scale=-1.0), reads=["fT0"], writes=["fT1"])
            S.op("act", lambda e: e.activation(out=tsp, in_=tsp, func=AF.Ln, bias=epsT[:, 3:4], scale=1.0),
                 reads=["fT1", "epsT"], writes=["fT1"])
            S.op("act", lambda e: e.activation(out=lw, in_=tsp, func=AF.Exp, bias=epsT[:, 2:3], scale=-1.0),
                 reads=["fT1", "epsT"], writes=["fT0"])
            LTLE = cst[:, CST_LT * 128:(CST_LE + 1) * 128]
            for c in range(4):
                S.op("pe", (lambda c: lambda e: e.matmul(
                    ps[4 + c // 2][:, (c % 2) * 256:(c % 2) * 256 + 256], lhsT=lw[:, c * 128:(c + 1) * 128], rhs=LTLE,
                    start=True, stop=True))(c), reads=["fT0", "cst"], writes=["ps%d" % (4 + c // 2)])
            G = lambda b: ps[4 + b][:, :].rearrange("p (c t) -> p c t", c=2)
            Lx = lambda b: G(b)[:, :, 0:128]
            Lg = lambda b: G(b)[:, :, 128:256]
            E = [V4(tmpf[0][:]), V4(tmpf[1][:])]
            for b in range(2):
                S.op("dve", (lambda b: lambda e: e.tensor_scalar(
                    out=nLgC[:, 2 * b:2 * b + 2], in0=G(b)[:, :, 255], scalar1=-1.0, scalar2=None, op0=ALU.mult))(b),
                    reads=["ps%d" % (4 + b)], writes=["nLgC"])
            S.op("act", lambda e: e.activation(out=gamC[:], in_=nLgC[:], func=AF.Exp), reads=["nLgC"], writes=["gamC"])

            def expE(i, src, scale, nm):
                for b in range(2):
                    S.op("act", (lambda b: lambda e: e.activation(out=E[i][:, 2 * b:2 * b + 2, :], in_=src(b),
                                                                  func=AF.Exp, scale=scale))(b),
                         reads=["ps%d" % (4 + b)], writes=[nm])

            expE(0, Lg, -1.0, "tmpf0")
            S.op("dve", lambda e: e.tensor_tensor(out=rt_bd[HA, :, 0:128], in0=r4[HA], in1=E[0][HA], op=ALU.mult),
                 reads=rn_ + ["tmpf0"], writes=[rtn])
            S.op("dve", lambda e: e.tensor_tensor(out=rt_bd[HB, :, 128:256], in0=r4[HB], in1=E[0][HB], op=ALU.mult),
                 reads=rn_ + ["tmpf0"], writes=[rtn])
            expE(1, Lg, 1.0, "tmpf1")
            S.op("dve", lambda e: e.tensor_tensor(out=kt, in0=k4, in1=E[1], op=ALU.mult),
                 reads=kn_ + ["tmpf1"], writes=[ktn])
            S.op("dve", lambda e: e.tensor_tensor(out=bt, in0=b4, in1=E[1], op=ALU.mult),
                 reads=bn_ + ["tmpf1"], writes=[btn])
            expE(0, Lx, -1.0, "tmpf0")
            S.op("dve", lambda e: e.scalar_tensor_tensor(out=at_bd[HA, :, 0:128], in0=kkn4[HA], scalar=-1.0,
                                                         in1=E[0][HA], op0=ALU.mult, op1=ALU.mult),
                 reads=kknn_ + ["tmpf0"], writes=[atn])
            S.op("dve", lambda e: e.scalar_tensor_tensor(out=at_bd[HB, :, 128:256], in0=kkn4[HB], scalar=-1.0,
                                                         in1=E[0][HB], op0=ALU.mult, op1=ALU.mult),
                 reads=kknn_ + ["tmpf0"], writes=[atn])
            for c in range(4):
                S.op("act", (lambda c: lambda e: e.activation(
                    out=E[1][:, c, :], in_=G(c // 2)[:, c % 2, 128:256], func=AF.Exp,
                    bias=nLgC[:, c:c + 1], scale=1.0))(c), reads=["ps%d" % (4 + c // 2), "nLgC"], writes=["tmpf1"])
            S.op("dve", lambda e: e.tensor_tensor(out=Kh, in0=k4, in1=E[1], op=ALU.mult),
                 reads=kn_ + ["tmpf1"], writes=["xnT4"])
            S.op("dve", lambda e: e.tensor_tensor(out=Bh, in0=b4, in1=E[1], op=ALU.mult),
                 reads=bn_ + ["tmpf1"], writes=["xnT5"])
            for src, sn, dst, dn in ((Kh, "xnT4", Kht, "xnT6"), (Bh, "xnT5", Bht, "xnT7")):
                for c in range(4):
                    S.op("pe", (lambda c, src: lambda e: e.transpose(
                        out=psb[7][:, c * 128:(c + 1) * 128], in_=src[:, c, :], identity=cB(CST_IDENT)))(c, src),
                        reads=[sn, "cstb"], writes=["ps7"])
                S.op("act", (lambda dst: lambda e: e.copy(out=dst, in_=psb[7][:, 0:512]))(dst),
                     reads=["ps7"], writes=[dn])
            mLT = cB(CST_LT).unsqueeze(1).broadcast_to([128, 4, 128])
            mLE = cB(CST_LE).unsqueeze(1).broadcast_to([128, 4, 128])
            mGT = cB(CST_GT).unsqueeze(1).broadcast_to([128, 4, 128])
            mID = cB(CST_IDENT).unsqueeze(1).broadcast_to([128, 4, 128])
            P4 = lambda b: ps[b][:, :].rearrange("p (h t) -> p h t", h=4)
            def half_gen(h4):
                A, B, Mk = halfA[h4], halfB[h4], halfK[h4]
                An, Bn, Mkn = "halfA%d" % h4, "halfB%d" % h4, "halfK%d" % h4
                Tf = Tfin[:, 4 * h4:4 * h4 + 4, :]
                Tn = "Tfin%d" % h4
                pb = [2 + 3 * h4 - (0 if h4 == 0 else 1) + i for i in range(3)]
                pb = [4, 5, 6]

                def mm_pair(bank, lf, rbd, rd):
                    for cp in range(2):
                        c = 2 * h4 + cp
                        S.op("pe", (lambda cp, c: lambda e: e.matmul(
                            ps[bank][:, cp * 256:(cp + 1) * 256], lhsT=lf[:, c, :], rhs=rbd[:, c, :],
                            start=True, stop=True))(cp, c), reads=rd, writes=["ps%d" % bank])

                def mm_pairT(bank, lbd, rf, rd):
                    for i in range(4):
                        c, hf = 2 * h4 + i // 2, i % 2
                        S.op("pe", (lambda i, c, hf: lambda e: e.matmul(
                            ps[bank][:, i * 128:(i + 1) * 128], lhsT=lbd[:, c, hf * 128:(hf + 1) * 128], rhs=rf[:, c, :],
                            start=True, stop=True))(i, c, hf), reads=rd, writes=["ps%d" % bank])

                mm_pair(pb[0], bt, at_bd, [btn, atn])
                S.op("dve", (lambda A, b0: lambda e: e.tensor_tensor(out=A[:], in0=P4(b0), in1=mLT, op=ALU.mult))(A, pb[0]),
                     reads=["ps%d" % pb[0], "cstb"], writes=[An])
                yield
                mm_pairT(pb[1], at_bd, bt, [btn, atn])
                S.op("dve", (lambda B, b1: lambda e: e.tensor_tensor(out=B[:], in0=P4(b1), in1=mGT, op=ALU.mult))(B, pb[1]),
                     reads=["ps%d" % pb[1], "cstb"], writes=[Bn])
                yield
                mm_pair(pb[2], kt, at_bd, [ktn, atn])
                S.op("dve", (lambda Mk, b2: lambda e: e.tensor_tensor(out=Mk[:], in0=P4(b2), in1=mLT, op=ALU.mult))(Mk, pb[2]),
                     reads=["ps%d" % pb[2], "cstb"], writes=[Mkn])
                yield
                mm_pair(pb[0], bt, rt_bd, [btn, rtn])
                S.op("dve", (lambda b0, h4: lambda e: e.tensor_tensor(out=Mbr[:, 4 * h4:4 * h4 + 4, :], in0=P4(b0), in1=mLE,
                                                                      op=ALU.mult))(pb[0], h4),
                     reads=["ps%d" % pb[0], "cstb"], writes=["Mbr%d" % h4])
                yield
                mm_pair(pb[1], kt, rt_bd, [ktn, rtn])
                S.op("dve", (lambda b1, h4: lambda e: e.tensor_tensor(out=Mkr[:, 4 * h4:4 * h4 + 4, :], in0=P4(b1), in1=mLE,
                                                                      op=ALU.mult))(pb[1], h4),
                     reads=["ps%d" % pb[1], "cstb"], writes=["Mkr%d" % h4])
                yield
                for i in range(4):
                    h = 4 * h4 + i
                    S.op("pe", (lambda i, h, Mk: lambda e: e.matmul(
                        ps[7][:, h * 64:(h + 1) * 64], lhsT=Mk[:, i, :], rhs=vtok[:, tb, h * 64:(h + 1) * 64],
                        start=True, stop=True))(i, h, Mk), reads=[Mkn] + GB0, writes=["ps7"])
                S.op("dve", (lambda A, Tf: lambda e: e.tensor_tensor(out=Tf, in0=A[:], in1=mID, op=ALU.add))(A, Tf),
                     reads=[An, "cstb"], writes=[Tn])
                for lvl in range(1, 7):
                    if lvl <= 5:
                        for i in range(4):
                            S.op("pe", (lambda i, A, B, b0: lambda e: e.matmul(
                                ps[b0][:, i * 128:(i + 1) * 128], lhsT=B[:, i, :], rhs=A[:, i, :],
                                start=True, stop=True))(i, A, B, pb[0]), reads=[An, Bn], writes=["ps%d" % pb[0]])
                    for i in range(4):
                        S.op("pe", (lambda i, A, B, b1: lambda e: e.matmul(
                            ps[b1][:, i * 128:(i + 1) * 128], lhsT=A[:, i, :], rhs=B[:, i, :],
                            start=True, stop=True))(i, A, B, pb[1]), reads=[An, Bn], writes=["ps%d" % pb[1]])
                    if lvl <= 5:
                        if T >= 4:
                            S.op("dve", (lambda A, b0: lambda e: e.tensor_copy(out=A[:], in_=P4(b0)))(A, pb[0]),
                                 reads=["ps%d" % pb[0]], writes=[An])
                        else:
                            S.op("act", (lambda A, b0: lambda e: e.copy(out=A[:], in_=P4(b0)))(A, pb[0]),
                                 reads=["ps%d" % pb[0]], writes=[An])
                    if T < 4:
                        S.op("act", (lambda B, b1: lambda e: e.copy(out=B[:], in_=P4(b1)))(B, pb[1]),
                             reads=["ps%d" % pb[1]], writes=[Bn])
                    else:
                        S.op("dve", (lambda B, b1: lambda e: e.tensor_copy(out=B[:], in_=P4(b1)))(B, pb[1]),
                             reads=["ps%d" % pb[1]], writes=[Bn])
                    for i in range(4):
                        S.op("pe", (lambda i, B, Tf, b2: lambda e: e.matmul(
                            ps[b2][:, i * 128:(i + 1) * 128], lhsT=B[:, i, :], rhs=Tf[:, i, :],
                            start=True, stop=True))(i, B, Tf, pb[2]), reads=[Bn, Tn], writes=["ps%d" % pb[2]])
                    S.op("dve", (lambda Tf, b2: lambda e: e.tensor_tensor(out=Tf, in0=Tf, in1=P4(b2), op=ALU.add))(Tf, pb[2]),
                         reads=["ps%d" % pb[2], Tn], writes=[Tn])
                    yield
            gens = [half_gen(0), half_gen(1)]
            while gens:
                for g_ in list(gens):
                    try:
                        next(g_)
                    except StopIteration:
                        gens.remove(g_)
            S.op("act", lambda e: e.copy(out=Et, in_=ps[7][:, :]), reads=["ps7"], writes=["fT2"])
            Xt, Ut = silb[0], silb[1]
            HS = lambda h: slice(64 * (h % 2), 64 * (h % 2) + 64)
            for h in range(8):
                S.op("pe", (lambda h: lambda e: e.matmul(
                    ps[4][:, h * 64:(h + 1) * 64], lhsT=at_bd[:, h // 2, (h % 2) * 128:(h % 2) * 128 + 128],
                    rhs=Sb[:, h // 2, :], start=True, stop=True))(h), reads=[atn, "Sb"], writes=["ps4"])
            S.op("dve", lambda e: e.tensor_tensor(out=Xt[:], in0=ps[4][:, :], in1=Et, op=ALU.add),
                 reads=["ps4", "fT2"], writes=["silb0"])
            for h in range(8):
                S.op("pe", (lambda h: lambda e: e.matmul(
                    ps[5][:, h * 64:(h + 1) * 64], lhsT=Tfin[:, h, :], rhs=Xt[:, h * 64:(h + 1) * 64],
                    start=True, stop=True))(h), reads=["Tfin%d" % (h // 4), "silb0"], writes=["ps5"])
            S.op("act", lambda e: e.copy(out=Ut[:], in_=ps[5][:, :]), reads=["ps5"], writes=["silb1"])
            for h in range(8):
                hc = slice(h * 64, (h + 1) * 64)
                S.op("pe", (lambda h, hc: lambda e: e.matmul(
                    ps[6][:, hc], lhsT=rt_bd[:, h // 2, (h % 2) * 128:(h % 2) * 128 + 128], rhs=Sb[:, h // 2, :],
                    start=True, stop=False))(h, hc),
                    reads=[rtn, "Sb"], writes=["ps6"])
                S.op("pe", (lambda h, hc: lambda e: e.matmul(
                    ps[6][:, hc], lhsT=Mbr[:, h, :], rhs=Ut[:, hc], start=False, stop=False))(h, hc),
                    reads=["Mbr%d" % (h // 4), "silb1"], writes=["ps6"])
                S.op("pe", (lambda h, hc: lambda e: e.matmul(
                    ps[6][:, hc], lhsT=Mkr[:, h, :], rhs=vtok[:, tb, hc], start=False, stop=True))(h, hc),
                    reads=["Mkr%d" % (h // 4)] + GB0, writes=["ps6"])
            S.op("act", lambda e: e.copy(out=y, in_=ps[6][:, :]), reads=["ps6"], writes=["fT3"])
            for h in range(8):
                hc = slice(h * 64, (h + 1) * 64)
                pc = slice((h // 2) * 128, (h // 2) * 128 + 128)
                S.op("pe", (lambda hc, pc: lambda e: e.matmul(
                    ps[7][:, hc], lhsT=Bht[:, pc], rhs=Ut[:, hc], start=True, stop=False))(hc, pc),
                    reads=["xnT7", "silb1"], writes=["ps7"])
                S.op("pe", (lambda hc, pc: lambda e: e.matmul(
                    ps[7][:, hc], lhsT=Kht[:, pc], rhs=vtok[:, tb, hc], start=False, stop=True))(hc, pc),
                    reads=["xnT6"] + GB0, writes=["ps7"])
            S.op("dve", lambda e: e.tensor_tensor(out=Sf[:], in0=Sf[:], in1=gamC[:, :].unsqueeze(2).broadcast_to([128, 4, 64]),
                                                  op=ALU.mult), reads=["Sf", "gamC"], writes=["Sf"])
            for hf, rs in ((0, slice(0, 64)), (1, slice(64, 128))):
                S.op("dve", (lambda hf, rs: lambda e: e.tensor_tensor(
                    out=Sf[rs], in0=Sf[rs],
                    in1=ps[7][rs, :].rearrange("p (c h v) -> p c h v", c=4, h=2)[:, :, hf, :], op=ALU.add))(hf, rs),
                    reads=["Sf", "ps7"], writes=["Sf"])
            S.op("dve", lambda e: e.tensor_copy(out=Sb[:], in_=Sf[:]), reads=["Sf"], writes=["Sb"])
            y3 = y.rearrange("p (h j) -> p h j", h=8)
            yc = fTc(4).rearrange("p (h j) -> p h j", h=8)
            ysq = fTc(5).rearrange("p (h j) -> p h j", h=8)
            s1, s2 = gst[:, 0, :], gst[:, 1, :]
            bc8 = lambda a: a.unsqueeze(2).broadcast_to([128, 8, 64])
            S.op("dve", lambda e: e.tensor_reduce(out=s1, in_=y3, axis=AX.X, op=ALU.add), reads=["fT3"], writes=["gst"])
            S.op("dve", lambda e: e.tensor_scalar(out=s1, in0=s1, scalar1=1.0 / 64, scalar2=None, op0=ALU.mult),
                 reads=["gst"], writes=["gst"])
            S.op("dve", lambda e: e.tensor_tensor(out=yc, in0=y3, in1=bc8(s1), op=ALU.subtract),
                 reads=["fT3", "gst"], writes=["fT4"])
            S.op("dve", lambda e: e.tensor_tensor(out=ysq, in0=yc, in1=yc, op=ALU.mult), reads=["fT4"], writes=["fT5"])
            S.op("dve", lambda e: e.tensor_reduce(out=s2, in_=ysq, axis=AX.X, op=ALU.add), reads=["fT5"], writes=["gst"])
            S.op("act", lambda e: e.activation(out=s2, in_=s2, func=AF.Ln, bias=epsT[:, 1:2], scale=1.0 / 64),
                 reads=["gst", "epsT"], writes=["gst"])
            S.op("act", lambda e: e.activation(out=s2, in_=s2, func=AF.Exp, scale=-0.5), reads=["gst"], writes=["gst"])
            S.op("dve", lambda e: e.tensor_tensor(out=yc, in0=yc, in1=bc8(s2), op=ALU.mult),
                 reads=["fT4", "gst"], writes=["fT4"])
            S.op("dve", lambda e: e.tensor_tensor(out=fTc(4), in0=fTc(4), in1=gng[:], op=ALU.mult),
                 reads=["fT4", "gng"], writes=["fT4"])
            S.op("dve", lambda e: e.tensor_tensor(out=fTc(4), in0=fTc(4), in1=gnb[:], op=ALU.add),
                 reads=["fT4", "gnb"], writes=["fT4"])
            S.op("dve", lambda e: e.tensor_tensor(out=ysq, in0=vtok[:, tb, :].rearrange("p (h j) -> p h j", h=8),
                                                  in1=bc8(bonus[:, tb, :]), op=ALU.mult),
                 reads=GB0 + ["bonus"], writes=["fT5"])
            S.op("dve", lambda e: e.tensor_tensor(out=fTc(4), in0=fTc(4), in1=fTc(5), op=ALU.add),
                 reads=["fT4", "fT5"], writes=["fT4"])
            S.op("pe", lambda e: e.matmul(ps[4][:, :], lhsT=sgT[:, tsl], rhs=gate_wb[:], start=True, stop=True),
                 reads=["hT21", "gate_wb"], writes=["ps4"])
            yfin = sqb[0]
            S.op("dve", lambda e: e.tensor_tensor(out=yfin[:], in0=fTc(4), in1=ps[4][:, :], op=ALU.mult),
                 reads=["fT4", "ps4"], writes=["sqb0"])
            for c in range(4):
                S.op("pe", (lambda c: lambda e: e.transpose(
                    out=psb[7][:, c * 128:(c + 1) * 128], in_=yfin[:, c * 128:(c + 1) * 128], identity=cB(CST_IDENT)))(c),
                    reads=["sqb0", "cstb"], writes=["ps7"])
            S.op("act", lambda e: e.copy(out=mixT[:, 4:8, tsl], in_=psb[7][:, 0:512].rearrange("p (c t) -> p c t", c=4)),
                 reads=["ps7"], writes=["mixT%d" % (4 + i) for i in range(4)])

        def mixer_out(T):
            m_reads = ["mixT%d" % i for i in range(8)]
            for gi in range(4):
                k = wload(scr["w_out"][gi], "w_out%d" % gi, 2048)
                wt = wslot[k][:, 0:2048].rearrange("p (c j) -> p c j", c=8)
                for j in range(2):
                    dc = gi * 2 + j
                    bd = 4 + dc % 2
                    for c in range(NCH):
                        S.op("pe", (lambda c, j, wt, bd: lambda e: e.matmul(
                            ps[bd][:, :], lhsT=wt[:, c, j * 128:(j + 1) * 128], rhs=mixT[:, c, :],
                            start=(c == 0), stop=(c == NCH - 1)))(c, j, wt, bd),
                            reads=["wslot%d" % k] + m_reads, writes=["ps%d" % bd])
                    out_chunk_epilogue(dc, bd)
                    if dc >= 1:
                        out_chunk_stats(dc - 1)
            out_chunk_stats(NCH - 1)
            post_norm_residual(1)

        def mixer(T):
            pre_norm("mix_pre_g")
            S.begin_record()
            mixer_proj_sb(T)
            rec_p = S.end_record()
            S.begin_record()
            rwkv_prep(T)
            rec_q = S.end_record()
            S.replay_merged(rec_p, rec_q, cost_a={"pe": 0.4, "act": 0.6, "dve": 0.5, "pool": 0.5},
                            cost_b={"pe": 0.4, "act": 0.6, "dve": 0.65, "pool": 0.3})
            rwkv_prep_tail()
            S.begin_record()
            sb_attention(T)
            rec_a = S.end_record()
            S.begin_record()
            S.op("pool", lambda e: e.memset(fT[:, 6:8, :], 0.0), writes=["fT6", "fT7"])
            for tb in range(4):
                rwkv_chunk(T, tb)
            rec_b = S.end_record()
            S.replay_merged(rec_a, rec_b, cost_a={"pe": 0.45, "act": 0.6, "dve": 0.45, "pool": 0.5},
                            cost_b={"pe": 0.13, "act": 0.55, "dve": 0.65, "pool": 0.9})
            if dbg:
                S.dma("sp", (lambda T: lambda e: e.dma_start(out=dbg_d[T], in_=mixT[:].rearrange("p c t -> p (c t)")))(T),
                      "dbg", reads=["mixT%d" % i for i in range(8)], writes=["dbg_hbm"])
            mixer_out(T)

        io_tile(-1, 0)
        for T in range(n_tiles):
            if stage in ("ffn1", "ffn12", "full"):
                ffn("ffn1", "ffn1_pre_g", 0)
            if stage in ("mix", "full"):
                mixer(T)
            if stage in ("ffn12", "full"):
                ffn("ffn2", "ffn2_pre_g", 2)
            io_tile(T, T + 1)
        S.final_wait("sp", ["out_hbm"] + (["dbg_hbm"] if dbg else []))
        S.emit(nc, st)
    return nc


def host_inputs(inputs, b):
    m = {"x": np.ascontiguousarray(inputs["x"][b]), "cst": make_consts()}
    ppa = np.zeros((128, NPP), np.float32)
    for name, (o, n) in PP.items():
        v = np.asarray(inputs[name], np.float32).reshape(-1)
        ppa[:, o:o + n] = v.reshape(n, 128).T
    m["pp"] = ppa
    for f in ("ffn1", "ffn2"):
        for w in ("_w_gate", "_w_up", "_w_down"):
            m[f + w] = np.ascontiguousarray(np.asarray(inputs[f + w], np.float32)[0])
    for nm in ("w_in", "w_out", "decay_w2", "iclr_a2", "gate_w2"):
        m[nm] = np.ascontiguousarray(np.asarray(inputs[nm], np.float32)[0])
    for nm in ("decay_w0", "gn_g", "gn_b"):
        m[nm] = np.ascontiguousarray(np.asarray(inputs[nm], np.float32).reshape(1, 512))
    return m


def kernel(**inputs):
    inputs = {k: np.asarray(v) for k, v in inputs.items()}
    nc = build_nc()
    in_maps = [host_inputs(inputs, b) for b in range(8)]
    res = run_bass_kernel_spmd(nc, in_maps, core_ids=list(range(8)))
    return np.stack([np.asarray(r["out"], np.float32) for r in res.results], axis=0)
```
